# Optimizing a Trainium2 kernel written in Bass

```python
import jax, jax.numpy as jnp
from jax import lax
import numpy as np

D_MODEL = 2048
BATCH = 16
SEQ = 256
DEPTH = 4
DEC_BATCH = 8
DEC_SEQ = 1024
PAST_LEN = 256

GRID_W = 64
N_EVEN = (DEPTH + 1) // 2
N_ODD = DEPTH // 2
D_RNN = D_MODEL // 2
N_RNN_BLOCKS = 8
RNN_BW = D_RNN // N_RNN_BLOCKS
D_CONV = D_MODEL // 2
RG_CONV_W = 4
RG_CONV_LEFT = 2
CF_CONV_W = 31
CF_CONV_LEFT = (CF_CONV_W - 1) // 2
LRU_C = 8.0
N_HEADS = 16
HEAD_DIM = D_MODEL // N_HEADS
WIN_R = 8
WIN_C = 16
Q_BLOCK = 128
D_FF = 5632
FFN_CONV_W = 3
FFN_CONV_LEFT = 1
EPS = 1e-6
NEG_INF = -1e30

kernel_name = "hybrid_diffusion_rglru_conformer_natten_step"


def rms_norm(x, g):
    xf = x.astype(jnp.float32)
    y = xf * lax.rsqrt(jnp.mean(xf * xf, axis=-1, keepdims=True) + EPS)
    return (y * g).astype(x.dtype)


def layer_norm(x, g, b):
    xf = x.astype(jnp.float32)
    mu = jnp.mean(xf, axis=-1, keepdims=True)
    xc = xf - mu
    y = xc * lax.rsqrt(jnp.mean(xc * xc, axis=-1, keepdims=True) + EPS)
    return (y * g + b).astype(x.dtype)


def dwconv(x, w, b, left):
    K, C = w.shape
    y = lax.conv_general_dilated(x, w[:, None, :], window_strides=(1,), padding=[(left, K - 1 - left)],
                                 dimension_numbers=('NWC', 'WIO', 'NWC'), feature_group_count=C)
    return y + b


def ada_mod(cond, w, b):
    m = jax.nn.silu(cond) @ w + b
    return jnp.split(m[:, None, :], 6, axis=-1)


def _lin_combine(e1, e2):
    a1, b1 = e1
    a2, b2 = e2
    return a1 * a2, a2 * b1 + b2


def rg_lru(xc, w_gate, b_gate, lam, h0):
    B, T, W = xc.shape
    xf = xc.astype(jnp.float32)
    xb = xf.reshape(B, T, N_RNN_BLOCKS, RNN_BW)
    gl = jnp.einsum('btnk,dgnkj->dgbtnj', xb, w_gate.astype(jnp.float32)).reshape(2, 2, B, T, W)
    gl = gl + b_gate.astype(jnp.float32)[:, :, None, None, :]
    r = jax.nn.sigmoid(gl[:, 0])
    i = jax.nn.sigmoid(gl[:, 1])
    log_a = -LRU_C * r * jax.nn.softplus(-lam.astype(jnp.float32))[:, None, None, :]
    a = jnp.exp(log_a)
    bx = jnp.sqrt(-jnp.expm1(2.0 * log_a)) * i * xf[None]
    a = jnp.stack([a[0], a[1, :, ::-1]])
    bx = jnp.stack([bx[0], bx[1, :, ::-1]])
    bx = bx.at[:, :, 0].add(a[:, :, 0] * h0.astype(jnp.float32))
    _, h = lax.associative_scan(_lin_combine, (a, bx), axis=2)
    final = h[:, :, -1]
    y = h[0] + h[1, :, ::-1]
    return y.astype(xc.dtype), final


def even_mixer(h, w_in, rg_cw, rg_cb, rg_wg, rg_bg, rg_lam, cf_cw, cf_cb, cf_g, cf_b, w_out, h0):
    p = h @ w_in
    xr, gr, ca, cb = jnp.split(p, [D_RNN, 2 * D_RNN, 2 * D_RNN + D_CONV], axis=-1)
    xc = dwconv(xr, rg_cw, rg_cb, RG_CONV_LEFT)
    y_rnn, final = rg_lru(xc, rg_wg, rg_bg, rg_lam, h0)
    y_rnn = y_rnn * jax.nn.gelu(gr)
    u = ca * jax.nn.sigmoid(cb)
    u = jax.nn.silu(layer_norm(dwconv(u, cf_cw, cf_cb, CF_CONV_LEFT), cf_g, cf_b))
    return jnp.concatenate([y_rnn, u], axis=-1) @ w_out, final


def qkv_proj(h, w_qkv):
    B, T, _ = h.shape
    qkv = (h @ w_qkv).reshape(B, T, 3, N_HEADS, HEAD_DIM)
    return qkv[:, :, 0] * (HEAD_DIM ** -0.5), qkv[:, :, 1], qkv[:, :, 2]


def ctx_attention(q, k, v):
    B, S, H, Dh = q.shape
    nb = S // Q_BLOCK
    qb = q.reshape(B, nb, Q_BLOCK, H, Dh).transpose(1, 0, 2, 3, 4)

    def blk(qi):
        s = jnp.einsum('bqhd,bkhd->bhqk', qi, k).astype(jnp.float32)
        p = jax.nn.softmax(s, axis=-1).astype(v.dtype)
        return jnp.einsum('bhqk,bkhd->bqhd', p, v)

    o = lax.map(blk, qb)
    return o.transpose(1, 0, 2, 3, 4).reshape(B, S, H * Dh)


def na_attention(q, k, v, k_ctx, v_ctx, rel_bias):
    B, T, H, Dh = q.shape
    rows = T // GRID_W
    wr = min(WIN_R, rows)
    qg = q.reshape(B, rows, GRID_W, H, Dh)
    kg = k.reshape(B, rows, GRID_W, H, Dh)
    vg = v.reshape(B, rows, GRID_W, H, Dh)
    r = jnp.arange(rows)
    row_start = jnp.clip(r - wr // 2, 0, rows - wr)
    row_idx = row_start[:, None] + jnp.arange(wr)[None, :]
    k_band = kg[:, row_idx]
    v_band = vg[:, row_idx]
    cq = jnp.arange(GRID_W)
    col_start = jnp.clip(cq - WIN_C // 2, 0, GRID_W - WIN_C)
    col_mask = (cq[None, :] >= col_start[:, None]) & (cq[None, :] < col_start[:, None] + WIN_C)
    dr = row_idx - r[:, None] + (WIN_R - 1)
    dc = jnp.clip(cq[None, :] - cq[:, None], -(WIN_C - 1), WIN_C - 1) + (WIN_C - 1)
    bias = rel_bias[:, dr[:, :, None, None], dc[None, None]]
    bias = bias.transpose(0, 1, 3, 2, 4)
    s_win = jnp.einsum('brqhd,brikhd->bhrqik', qg, k_band).astype(jnp.float32) + bias[None]
    s_win = jnp.where(col_mask[None, None, None, :, None, :], s_win, NEG_INF)
    s_win = s_win.reshape(B, H, rows, GRID_W, wr * GRID_W)
    s_ctx = jnp.einsum('brqhd,bkhd->bhrqk', qg, k_ctx).astype(jnp.float32)
    p = jax.nn.softmax(jnp.concatenate([s_win, s_ctx], axis=-1), axis=-1).astype(v.dtype)
    p_win = p[..., :wr * GRID_W].reshape(B, H, rows, GRID_W, wr, GRID_W)
    p_ctx = p[..., wr * GRID_W:]
    o = jnp.einsum('bhrqik,brikhd->brqhd', p_win, v_band) + jnp.einsum('bhrqk,bkhd->brqhd', p_ctx, v_ctx)
    return o.reshape(B, T, H * Dh)


def conv_ffn(h, w_up, cw, cb, w_down):
    u = dwconv(h @ w_up, cw, cb, FFN_CONV_LEFT)
    g, val = jnp.split(u, 2, axis=-1)
    return (jax.nn.gelu(g) * val) @ w_down


def pre_mod(x, g, shift, scale):
    return rms_norm(x, g) * (1 + scale) + shift


def setup_inputs(seed: int = 0) -> dict:
    key = jax.random.key(seed)
    ks = jax.random.split(key, 32)
    f32 = jnp.float32

    def nrm(k, shape, std):
        return std * jax.random.normal(k, shape, f32)

    u = jax.random.uniform(ks[15], (N_EVEN, 2, D_RNN), f32, 0.9, 0.999)
    p = u ** (1.0 / LRU_C)
    return {
        'x_prompt': nrm(ks[0], (BATCH, SEQ, D_MODEL), 1.0),
        'x_sample': nrm(ks[1], (DEC_BATCH, DEC_SEQ, D_MODEL), 1.0),
        'cache_k': nrm(ks[2], (DEC_BATCH, N_ODD, PAST_LEN, N_HEADS, HEAD_DIM), 1.0),
        'cache_v': nrm(ks[3], (DEC_BATCH, N_ODD, PAST_LEN, N_HEADS, HEAD_DIM), 1.0),
        'state_rglru': nrm(ks[4], (DEC_BATCH, N_EVEN, 2, D_RNN), 0.5),
        'c': nrm(ks[5], (DEC_BATCH, D_MODEL), 1.0),
        'c_ctx': nrm(ks[6], (D_MODEL,), 1.0),
        'w_ada': nrm(ks[7], (DEPTH, D_MODEL, 6 * D_MODEL), 0.5 * D_MODEL ** -0.5),
        'b_ada': nrm(ks[8], (DEPTH, 6 * D_MODEL), 0.01),
        'norm_g': 1.0 + nrm(ks[9], (DEPTH, 4, D_MODEL), 0.02),
        'w_in_even': nrm(ks[10], (N_EVEN, D_MODEL, 2 * D_RNN + 2 * D_CONV), D_MODEL ** -0.5),
        'rg_conv_w': nrm(ks[11], (N_EVEN, RG_CONV_W, D_RNN), RG_CONV_W ** -0.5),
        'rg_conv_b': nrm(ks[12], (N_EVEN, D_RNN), 0.01),
        'rg_w_gate': nrm(ks[13], (N_EVEN, 2, 2, N_RNN_BLOCKS, RNN_BW, RNN_BW), RNN_BW ** -0.5),
        'rg_b_gate': nrm(ks[14], (N_EVEN, 2, 2, D_RNN), 0.01),
        'rg_lambda': jnp.log(p) - jnp.log1p(-p),
        'cf_conv_w': nrm(ks[16], (N_EVEN, CF_CONV_W, D_CONV), CF_CONV_W ** -0.5),
        'cf_conv_b': nrm(ks[17], (N_EVEN, D_CONV), 0.01),
        'cf_ln_g': 1.0 + nrm(ks[18], (N_EVEN, D_CONV), 0.02),
        'cf_ln_b': nrm(ks[19], (N_EVEN, D_CONV), 0.01),
        'w_out_even': nrm(ks[20], (N_EVEN, D_RNN + D_CONV, D_MODEL), (D_RNN + D_CONV) ** -0.5),
        'w_qkv_odd': nrm(ks[21], (N_ODD, D_MODEL, 3 * D_MODEL), D_MODEL ** -0.5),
        'na_rel_bias': nrm(ks[22], (N_ODD, N_HEADS, 2 * WIN_R - 1, 2 * WIN_C - 1), 0.1),
        'w_out_odd': nrm(ks[23], (N_ODD, D_MODEL, D_MODEL), D_MODEL ** -0.5),
        'w_ffn_up': nrm(ks[24], (DEPTH, D_MODEL, 2 * D_FF), D_MODEL ** -0.5),
        'ffn_conv_w': nrm(ks[25], (DEPTH, FFN_CONV_W, 2 * D_FF), FFN_CONV_W ** -0.5),
        'ffn_conv_b': nrm(ks[26], (DEPTH, 2 * D_FF), 0.01),
        'w_ffn_down': nrm(ks[27], (DEPTH, D_FF, D_MODEL), D_FF ** -0.5),
    }


def reference(x_prompt, x_sample, cache_k, cache_v, state_rglru, c, c_ctx, w_ada, b_ada, norm_g,
              w_in_even, rg_conv_w, rg_conv_b, rg_w_gate, rg_b_gate, rg_lambda, cf_conv_w, cf_conv_b,
              cf_ln_g, cf_ln_b, w_out_even, w_qkv_odd, na_rel_bias, w_out_odd, w_ffn_up, ffn_conv_w,
              ffn_conv_b, w_ffn_down):
    xp = x_prompt
    xs = x_sample
    new_k, new_v, new_rg = [], [], []
    for l in range(DEPTH):
        g = norm_g[l]
        mods_p = ada_mod(c_ctx[None, :], w_ada[l], b_ada[l])
        mods_s = ada_mod(c, w_ada[l], b_ada[l])
        hp = pre_mod(xp, g[0], mods_p[0], mods_p[1])
        hs = pre_mod(xs, g[0], mods_s[0], mods_s[1])
        if l % 2 == 0:
            e = l // 2
            ew = (w_in_even[e], rg_conv_w[e], rg_conv_b[e], rg_w_gate[e], rg_b_gate[e], rg_lambda[e],
                  cf_conv_w[e], cf_conv_b[e], cf_ln_g[e], cf_ln_b[e], w_out_even[e])
            h0_ctx = jnp.zeros((2, xp.shape[0], D_RNN), jnp.float32)
            op, fin = even_mixer(hp, *ew, h0_ctx)
            new_rg.append(fin.transpose(1, 0, 2))
            os_, _ = even_mixer(hs, *ew, state_rglru[:, e].transpose(1, 0, 2))
        else:
            o = l // 2
            qp, kp, vp = qkv_proj(hp, w_qkv_odd[o])
            new_k.append(kp)
            new_v.append(vp)
            op = ctx_attention(qp, kp, vp) @ w_out_odd[o]
            qs, ks_, vs = qkv_proj(hs, w_qkv_odd[o])
            os_ = na_attention(qs, ks_, vs, cache_k[:, o], cache_v[:, o], na_rel_bias[o]) @ w_out_odd[o]
        xp = xp + mods_p[2] * rms_norm(op, g[1])
        xs = xs + mods_s[2] * rms_norm(os_, g[1])
        fp = conv_ffn(pre_mod(xp, g[2], mods_p[3], mods_p[4]), w_ffn_up[l], ffn_conv_w[l], ffn_conv_b[l], w_ffn_down[l])
        fs = conv_ffn(pre_mod(xs, g[2], mods_s[3], mods_s[4]), w_ffn_up[l], ffn_conv_w[l], ffn_conv_b[l], w_ffn_down[l])
        xp = xp + mods_p[5] * rms_norm(fp, g[3])
        xs = xs + mods_s[5] * rms_norm(fs, g[3])
    new_cache_k = jnp.stack(new_k, axis=1)
    new_cache_v = jnp.stack(new_v, axis=1)
    new_state_rglru = jnp.stack(new_rg, axis=1)
    return (xp, xs, new_cache_k, new_cache_v, new_state_rglru)
```

```python
import os
from contextlib import ExitStack
import numpy as np
import concourse.bass as bass
import concourse.mybir as mybir
from concourse.bass_utils import run_bass_kernel_spmd

F32, BF16 = mybir.dt.float32, mybir.dt.bfloat16
AF = mybir.ActivationFunctionType
ALU = mybir.AluOpType
AX = mybir.AxisListType

NCORES = 8
L = 4
D = 2048
KC = 16
DFF = 5632
NFF = 44
GRP = 4
NH = 16
EPS = 1e-6
NRING = 7
WIN_ROWS = 10
NKW = WIN_ROWS * 64
NKS = NKW + 256
QT_R0 = [0, 0, 0, 2, 4, 6, 6, 6]
QT_VAR = [0, 1, 2, 2, 2, 2, 3, 4]


def param_layout():
    off = {}
    n = 0

    def add(name, *shape):
        nonlocal n
        off[name] = (n, shape)
        n += int(np.prod(shape))
    for l in range(L):
        add(f"b_ada{l}", 96)
        add(f"norm_g{l}", 4, 16)
        add(f"fcw{l}", 88, 3)
        add(f"fcb{l}", 88)
    for e in range(2):
        add(f"rcw{e}", 8, 4)
        add(f"rcb{e}", 8)
        add(f"rbg{e}", 2, 2, 8)
        add(f"lam{e}", 2, 8)
        add(f"ccw{e}", 8, 31)
        add(f"ccb{e}", 8)
        add(f"lng{e}", 8)
        add(f"lnb{e}", 8)
    return off, n


class Res:
    __slots__ = ("w", "r", "excl")

    def __init__(self, excl=False):
        self.w = None
        self.r = []
        self.excl = excl


class TB:
    def __init__(self, t, excl=False):
        self.t = t
        self.res = {}
        self.excl = excl

    def R(self, key=0):
        r = self.res.get(key)
        if r is None:
            r = self.res[key] = Res(self.excl)
        return r

    def __getitem__(self, k):
        return self.t[k]


class KB:
    def __init__(self, nc, es):
        self.nc = nc
        self.es = es
        self.eng = {"pe": nc.tensor, "act": nc.scalar, "dve": nc.vector, "pool": nc.gpsimd, "sp": nc.sync}
        self.sem = {e: es.enter_context(nc.semaphore("s_" + e)) for e in self.eng}
        self.cnt = {e: 0 for e in self.eng}
        self.seen = {e: {} for e in self.eng}
        self.dsems = []
        self.nd = 0

    def sb(self, name, shape, dt, es=None):
        self.nd += 1
        return TB((es or self.es).enter_context(self.nc.sbuf_tensor(f"{name}_{self.nd}", list(shape), dt)))

    def dsem(self, name, ring=False):
        self.nd += 1
        s = self.es.enter_context(self.nc.semaphore(f"{name}_{self.nd}"))
        h = {"s": s, "c": 0, "id": len(self.dsems), "ring": ring}
        self.dsems.append(h)
        return h

    def _wait(self, eng, dep):
        if dep[0] == "e":
            key = ("e", dep[1])
            semh = self.sem[dep[1]]
        else:
            key = ("d", dep[1]["id"])
            semh = dep[1]["s"]
        val = dep[2]
        if self.seen[eng].get(key, 0) >= val:
            return
        self.seen[eng][key] = val
        self.eng[eng].wait_ge(semh, val)

    def _deps(self, eng, reads, writes):
        for r in reads:
            if r.w is not None:
                self._wait_f(eng, r.w)
            if r.excl:
                for d in r.r:
                    if not (d[0] == "e" and d[1] == eng):
                        self._wait_f(eng, d)
        for w in writes:
            if w.w is not None:
                self._wait_f(eng, w.w)
            for d in w.r:
                self._wait_f(eng, d)

    def _wait_f(self, eng, d):
        if eng == "pe" and d[0] == "e" and d[1] == "pe":
            return
        self._wait(eng, d)

    def _mark(self, me, reads, writes):
        for w in writes:
            w.w = me
            w.r = []
        for r in reads:
            if r in writes:
                continue
            r.r.append(me)
            if len(r.r) > 24:
                last = {}
                for d in r.r:
                    k = (d[0], d[1] if d[0] == "e" else d[1]["id"])
                    if k not in last or last[k][2] < d[2]:
                        last[k] = d
                r.r = list(last.values())

    def op(self, eng, fn, reads=(), writes=()):
        self._deps(eng, reads, writes)
        ins = fn(self.eng[eng])
        ins.then_inc(self.sem[eng], 1)
        self.cnt[eng] += 1
        self._mark(("e", eng, self.cnt[eng]), reads, writes)

    def mm(self, fns, reads=(), writes=()):
        self._deps("pe", reads, writes)
        ins = None
        for f in fns:
            ins = f(self.nc.tensor)
        ins.then_inc(self.sem["pe"], 1)
        self.cnt["pe"] += 1
        self._mark(("e", "pe", self.cnt["pe"]), reads, writes)

    def dma(self, q, out, in_, ds, reads=(), writes=()):
        self._deps(q, reads, writes)
        self.eng[q].dma_start(out=out, in_=in_).then_inc(ds["s"], 16)
        ds["c"] += 16
        self._mark(("d", ds, ds["c"]), reads, writes)

    def barrier(self, full=False):
        for e in self.eng:
            if e == "pool" and not full:
                continue
            for e2 in self.eng:
                if e2 != e and self.cnt[e2] > 0 and (full or e2 != "pool"):
                    self._wait(e, ("e", e2, self.cnt[e2]))
            for ds in self.dsems:
                if ds["c"] > 0 and (full or not ds["ring"]):
                    self._wait(e, ("d", ds, ds["c"]))

    def act(self, out, in_, func, reads, writes, **kw):
        self.op("act", lambda e: e.activation(out=out, in_=in_, func=func, **kw), reads, writes)

    def tt(self, out, a, b, op, reads, writes, eng="dve"):
        self.op(eng, lambda e: e.tensor_tensor(out=out, in0=a, in1=b, op=op), reads, writes)

    def stt(self, out, in0, scalar, in1, op0, op1, reads, writes, eng="dve"):
        self.op(eng, lambda e: e.scalar_tensor_tensor(out=out, in0=in0, scalar=scalar, in1=in1, op0=op0, op1=op1),
                reads, writes)

    def ts(self, out, in0, s1, s2, op0, op1, reads, writes, eng="dve"):
        if s2 is None:
            self.op(eng, lambda e: e.tensor_scalar(out=out, in0=in0, scalar1=s1, scalar2=None, op0=op0), reads, writes)
        else:
            self.op(eng, lambda e: e.tensor_scalar(out=out, in0=in0, scalar1=s1, scalar2=s2, op0=op0, op1=op1),
                    reads, writes)

    def cp(self, out, in_, reads, writes, eng="dve"):
        self.op(eng, lambda e: e.tensor_copy(out=out, in_=in_), reads, writes)


class Ring:
    def __init__(self, kb, n, shape, dt, name, es=None):
        self.kb = kb
        self.n = n
        self.slots = [kb.sb(f"{name}{i}", shape, dt, es) for i in range(n)]
        self.ds = [kb.dsem(f"{name}_d{i}", ring=True) for i in range(n)]
        self.held = [False] * n
        self.plan = []
        self.issued = 0
        self.consumed = 0

    def _pump(self):
        while self.issued < len(self.plan):
            i = self.issued
            sl = i % self.n
            if self.held[sl]:
                break
            s = self.slots[sl]
            out, in_ = self.plan[i][1](s)
            self.kb.dma("pool", out, in_, self.ds[sl], writes=[s.R()])
            self.held[sl] = True
            self.issued += 1

    def get(self, tag):
        c = self.consumed
        assert self.plan[c][0] == tag, (c, self.plan[c][0], tag)
        self._pump()
        assert self.issued > c, ("ring starved", tag)
        self.consumed += 1
        return self.slots[c % self.n]

    def release(self, s):
        self.held[self.slots.index(s)] = False
        self._pump()


def build_program(nl=L, do_sample=True, do_prompt=True):
    nc = bass.Bass("TRN2", target_bir_lowering=False)
    poff, NP = param_layout()

    def din(name, shape):
        return nc.dram_tensor(name, list(shape), F32, kind="ExternalInput").ap()

    def dout(name, shape):
        return nc.dram_tensor(name, list(shape), F32, kind="ExternalOutput").ap()

    xp_d = din("xp", [128, KC, 512])
    xs_d = din("xs", [128, KC, 1024])
    cond_d = din("cond", [128, KC, 2])
    kctx_d = din("kctx", [2, 128, NH, 256])
    vctx_d = din("vctx", [2, 128, 2, D])
    h0_d = din("h0", [2, 128, 2, 8])
    par_d = din("par", [128, NP])
    btab_d = din("btab", [2, NH, 5, 128, NKW])
    w_ada = din("w_ada", [L, 96, 128, 16 * 128])
    w_in = din("w_in_even", [2, 32, 128, 16 * 128])
    w_gate = din("rg_w_gate", [2, 2, 2, 8, 128, 128])
    w_oute = din("w_out_even", [2, 16, 128, 16 * 128])
    w_qkv = din("w_qkv_odd", [2, 48, 128, 16 * 128])
    w_outo = din("w_out_odd", [2, 16, 128, 16 * 128])
    w_up = din("w_ffn_up", [L, 88, 128, 16 * 128])
    w_down = din("w_ffn_down", [L, DFF, D])

    yp_d = dout("yp", [128, KC, 512])
    ys_d = dout("ys", [128, KC, 1024])
    nk_d = dout("nk", [2, 128, NH, 512])
    nv_d = dout("nv", [2, 128, 4, D])
    nst_d = dout("nst", [2, 128, 2, 2, 8])

    es = ExitStack()
    with es:
        kb = KB(nc, es)
        PAR = kb.sb("PAR", [128, NP], F32)
        MODS = kb.sb("MODS", [128, L, 2, 96], F32)
        DER = kb.sb("DER", [128, 4, 16], F32)
        CD = kb.sb("CD", [128, 2, 2, 8], F32)
        CONDF = kb.sb("CONDF", [128, KC, 2], F32)
        CONDB = kb.sb("CONDB", [128, KC, 2], BF16)
        ONES = kb.sb("ONES", [128, 128], BF16)
        IDB = kb.sb("IDB", [128, 128], BF16)
        IDF = kb.sb("IDF", [128, 128], F32)
        NST = kb.sb("NST", [128, 2, 2, 2, 8], F32)
        ring = Ring(kb, NRING, [128, 2048], BF16, "ring")
        tplan = []
        PS = [TB(es.enter_context(nc.psum_tensor(f"ps{i}", [128, 1024], F32)), excl=True) for i in range(4)]
        ps_i = [0]

        def psget(n=4):
            t = PS[ps_i[0] % n]
            ps_i[0] += 1
            return t

        ds_misc = kb.dsem("misc")
        ds_gw = kb.dsem("gw")
        ds_x = kb.dsem("x")
        ds_out = kb.dsem("out")
        ds_ctx = kb.dsem("ctx")

        def P(name, *idx):
            o, shape = poff[name]
            strides = [int(np.prod(shape[i + 1:])) for i in range(len(shape))]
            for i, s in zip(idx, strides):
                o += i * s
            rem = int(np.prod(shape[len(idx):])) if len(idx) < len(shape) else 1
            return PAR[:, o:o + rem]

        def colblk(w3d, c0):
            src = w3d[c0 // 128]
            return lambda s: (s.t[:], src)

        def rowblk(w2d, r0):
            src = w2d[r0:r0 + 128, :]
            return lambda s: (s.t[:], src)

        plan = ring.plan
        ada_pre = [0] if do_sample else list(range(nl))
        for l in ada_pre:
            for cb in range(96):
                plan.append((("ada", l, cb), colblk(w_ada[l], cb * 128)))
        passes = ([("s", 1024, [(0, 1024)])] if do_sample else []) + \
                 ([("p", 512, [(0, 256), (256, 256)])] if do_prompt else [])

        def plan_ada(l2, cbs):
            for cb in cbs:
                plan.append((("ada", l2, cb), colblk(w_ada[l2], cb * 128)))
        for pname, T, seqs in passes:
            for l in range(nl):
                if l % 2 == 0:
                    e = l // 2
                    for n in range(8):
                        do_ada = pname == "s" and l + 1 < nl
                        plan.append((("win", pname, l, n, 0), colblk(w_in[e], 0 * 1024 + n * 128)))
                        if do_ada:
                            plan_ada(l + 1, range(n * 12, n * 12 + 6))
                        for part in range(1, 4):
                            plan.append((("win", pname, l, n, part), colblk(w_in[e], part * 1024 + n * 128)))
                        if do_ada:
                            plan_ada(l + 1, range(n * 12 + 6, n * 12 + 12))
                    for w in range(T // 512):
                        for m in range(16):
                            plan.append((("wout", pname, l, w, m), colblk(w_oute[e], m * 128)))
                else:
                    o = l // 2
                    for hd in range(NH):
                        for part in range(3):
                            plan.append((("wqkv", pname, l, hd, part), colblk(w_qkv[o], part * D + hd * 128)))
                            if pname == "s" and l + 1 < nl:
                                plan_ada(l + 1, range(hd * 6 + part * 2, hd * 6 + part * 2 + 2))
                        if pname == "s":
                            for qt in range(8):
                                src = btab_d[o, hd, QT_VAR[qt]]
                                tplan.append((("bt", l, hd, qt), (lambda src: lambda s: (s.t[:], src))(src)))
                    for w in range(T // 512):
                        for m in range(16):
                            plan.append((("wout", pname, l, w, m), colblk(w_outo[o], m * 128)))
                for w in range(T // 512):
                    for g in range(NFF // GRP):
                        for jj in range(GRP):
                            j = g * GRP + jj
                            plan.append((("wup", pname, l, w, j, 0), colblk(w_up[l], j * 128)))
                            plan.append((("wup", pname, l, w, j, 1), colblk(w_up[l], DFF + j * 128)))
                        for jj in range(GRP):
                            j = g * GRP + jj
                            plan.append((("wdn", pname, l, w, j), rowblk(w_down[l], j * 128)))

        def slot3(s):
            return s.t[:].rearrange("p (k n) -> p k n", k=16)

        SKIP = os.environ.get("MK_SKIP", "").split(",")
        ident_d = din("ident", [128, 128])
        EPSV = kb.sb("EPSV", [128, 2], F32)
        H0S = kb.sb("H0S", [128, 2, 8], F32)
        kplan, vplan = [], []
        if "units" in SKIP:
            tplan.clear()
        if do_sample and "units" not in SKIP:
            for l in range(nl):
                if l % 2 == 1:
                    o = l // 2
                    for hd in range(NH):
                        ksrc = kctx_d[o, :, hd, :]
                        vsrc = vctx_d[o, :, :, hd * 128:(hd + 1) * 128]
                        kplan.append((("kc", l, hd), (lambda src: lambda s: (s.t[:], src))(ksrc)))
                        vplan.append((("vc", l, hd), (lambda src: lambda s: (
                            s.t[:].rearrange("p (c d) -> p c d", c=2), src))(vsrc)))
        hc = [0]

        class PV:
            def __init__(self, tb, half):
                self.ap = tb.t[:, half * 512:(half + 1) * 512]
                self.res = Res(excl=True)

            def R(self):
                return self.res

            def __getitem__(self, k):
                return self.ap[k]
        PSH = [PV(PS[i // 2], i % 2) for i in range(8)]
        psh_i = [0]

        def pshget():
            t = PSH[4 + psh_i[0] % 2]
            psh_i[0] += 1
            return t

        def MM(out, lhsT, rhs, start=True, stop=True):
            return lambda pe: pe.matmul(out, lhsT=lhsT, rhs=rhs, start=start, stop=stop)

        kb.dma("sp", PAR[:], par_d, ds_misc, writes=[PAR.R()])
        kb.dma("sp", CONDF[:], cond_d, ds_misc, writes=[CONDF.R()])
        kb.dma("sp", IDF[:], ident_d, ds_misc, writes=[IDF.R()])
        for t_ in (PAR, CONDF, IDF):
            t_.R().w = ("d", ds_misc, ds_misc["c"])
        kb.op("dve", lambda e: e.memset(ONES[:], 1.0), writes=[ONES.R()])
        kb.op("dve", lambda e: e.memset(EPSV[:, 0:1], EPS), writes=[EPSV.R()])
        kb.op("dve", lambda e: e.memset(EPSV[:, 1:2], 1.0), writes=[EPSV.R()])
        kb.op("dve", lambda e: e.memset(NST[:], 0.0), writes=[NST.R()])
        kb.cp(IDB[:], IDF[:], [IDF.R()], [IDB.R()])
        kb.act(CONDB[:], CONDF[:], AF.Silu, [CONDF.R()], [CONDB.R()])

        def ada_run(l2, cbs, nps=4):
            for cb in cbs:
                s = ring.get(("ada", l2, cb))
                s3 = slot3(s)
                ps = psget(nps)
                kb.mm([MM(ps[:, 0:2], s3[:, k, :], CONDB[:, k, :], k == 0, k == 15) for k in range(16)],
                      [s.R(), CONDB.R()], [ps.R()])
                ring.release(s)
                kb.ts(MODS[:, l2, :, cb], ps[:, 0:2], P(f"b_ada{l2}")[:, cb:cb + 1], None, ALU.add, None,
                      [ps.R(), PAR.R()], [MODS.R((l2, 0)), MODS.R((l2, 1))])

        for l in ada_pre:
            ada_run(l, range(96))

        def derive(l, cnd):
            md = lambda m: MODS[:, l, cnd, m * 16:(m + 1) * 16]
            rd = [MODS.R((l, cnd)), PAR.R()]
            kb.stt(DER[:, 0, :], md(1), 1.0, P(f"norm_g{l}", 0), ALU.add, ALU.mult, rd, [DER.R()])
            kb.tt(DER[:, 1, :], md(2), P(f"norm_g{l}", 1), ALU.mult, rd, [DER.R()])
            kb.stt(DER[:, 2, :], md(4), 1.0, P(f"norm_g{l}", 2), ALU.add, ALU.mult, rd, [DER.R()])
            kb.tt(DER[:, 3, :], md(5), P(f"norm_g{l}", 3), ALU.mult, rd, [DER.R()])

        class NormTmp:
            def __init__(self, es_):
                self.RS = kb.sb("RS", [128, 512], F32, es_)
                self.SQ = [kb.sb(f"SQ{i}", [128, 512], BF16, es_) for i in range(4)]
                self.TM = [kb.sb(f"TM{i}", [128, 512], F32, es_) for i in range(2)]

        def rstd_of(getsrc, srcres, nt, getps=None):
            ps = (getps or psget)()
            for k in range(KC):
                sq = nt.SQ[k % 4]
                if k % 2 == 0:
                    kb.act(sq[:], getsrc(k), AF.Square, [srcres(k)], [sq.R()])
                else:
                    kb.tt(sq[:], getsrc(k), getsrc(k), ALU.mult, [srcres(k)], [sq.R()])
                kb.mm([MM(ps[:, 0:512], ONES[:], sq[:], k == 0, k == KC - 1)], [ONES.R(), sq.R()], [ps.R()])
            kb.act(nt.RS[:], ps[:, 0:512], AF.Ln, [ps.R(), EPSV.R()], [nt.RS.R()], scale=1.0 / D, bias=EPSV[:, 0:1])
            kb.act(nt.RS[:], nt.RS[:], AF.Exp, [nt.RS.R()], [nt.RS.R()], scale=-0.5)

        def premod(X, Hh, T, Avec, Bvec, extra_reads, nt, getps=None):
            for t in range(T // 512):
                cs = slice(t * 512, (t + 1) * 512)
                rstd_of(lambda k: X[:, k, cs], lambda k: X.R(k), nt, getps)
                for k in range(KC):
                    tm = nt.TM[k % 2]
                    kb.tt(tm[:], X[:, k, cs], nt.RS[:], ALU.mult, [X.R(k), nt.RS.R()], [tm.R()])
                    kb.act(Hh[:, k, cs], tm[:], AF.Identity, [tm.R(), DER.R()] + extra_reads, [Hh.R(k)],
                           scale=Avec[:, k:k + 1], bias=Bvec[:, k:k + 1])

        def post_residual(OB, X, cs, Gvec, nt, getps=None):
            rstd_of(lambda k: OB[:, k, :], lambda k: OB.R(k), nt, getps)
            for k in range(KC):
                tm = nt.TM[k % 2]
                kb.tt(tm[:], OB[:, k, :], nt.RS[:], ALU.mult, [OB.R(k), nt.RS.R()], [tm.R()])
                kb.stt(X[:, k, cs], tm[:], Gvec[:, k:k + 1], X[:, k, cs], ALU.mult, ALU.add,
                       [tm.R(), X.R(k), DER.R()], [X.R(k)])

        def proj_cols(s, Hh, T, ps):
            s3 = slot3(s)
            for t in range(T // 512):
                kb.mm([MM(ps[:, t * 512:(t + 1) * 512], s3[:, k, :], Hh[:, k, t * 512:(t + 1) * 512], k == 0, k == 15)
                       for k in range(16)], [s.R()] + [Hh.R(k) for k in range(16)], [ps.R()])

        def outproj_residual(pname, l, T, X, Y, OB, nt):
            for w in range(T // 512):
                cs = slice(w * 512, (w + 1) * 512)
                for m in range(16):
                    s = ring.get(("wout", pname, l, w, m))
                    s3 = slot3(s)
                    ps = psget()
                    kb.mm([MM(ps[:, 0:512], s3[:, k, :], Y[:, k, cs], k == 0, k == 15) for k in range(16)],
                          [s.R()] + [Y.R(k) for k in range(16)], [ps.R()])
                    ring.release(s)
                    kb.act(OB[:, m, :], ps[:, 0:512], AF.Copy, [ps.R()], [OB.R(m)])
                post_residual(OB, X, cs, DER[:, 1, :], nt)

        def even_mixer(pname, l, T, seqs, Hh, Y, tes):
            e = l // 2
            XC = kb.sb("XC", [128, T], F32, tes)
            XCB = kb.sb("XCB", [128, T], BF16, tes)
            HF = kb.sb("HF", [128, T], F32, tes)
            HB = kb.sb("HB", [128, T], F32, tes)
            A_ = kb.sb("A_", [128, T], F32, tes)
            BX = kb.sb("BX", [128, T], F32, tes)
            A2 = kb.sb("A2", [128, T], F32, tes)
            BX2 = kb.sb("BX2", [128, T], F32, tes)
            DG = [kb.sb("DG", [128, 128], BF16, tes) for _ in range(4)]
            gring = Ring(kb, 2, [128, 512], BF16, "gring", tes)
            for n_ in range(8):
                gsrc = w_gate[e][:, :, n_].rearrange("d g k j -> k (d g) j")
                gring.plan.append((("gw", n_), (lambda src: lambda s: (
                    s.t[:].rearrange("k (b j) -> k b j", b=4), src))(gsrc)))
            if pname == "s":
                kb.dma("sp", H0S[:], h0_d[e], ds_ctx, writes=[H0S.R()])
            lam = P(f"lam{e}")
            CDf = CD[:].rearrange("p a d n -> p a (d n)")
            kb.act(CDf[:, 0, :], lam, AF.Exp, [PAR.R()], [CD.R()], scale=-1.0)
            kb.act(CDf[:, 0, :], CDf[:, 0, :], AF.Ln, [CD.R(), EPSV.R()], [CD.R()], bias=EPSV[:, 1:2])
            kb.ts(CDf[:, 1, :], CDf[:, 0, :], -16.0, None, ALU.mult, None, [CD.R()], [CD.R()])
            kb.ts(CDf[:, 0, :], CDf[:, 0, :], -8.0, None, ALU.mult, None, [CD.R()], [CD.R()])
            for n in range(8):
                s = ring.get(("win", pname, l, n, 0))
                ps = psget()
                proj_cols(s, Hh, T, ps)
                ring.release(s)
                do_ada = pname == "s" and l + 1 < nl
                if do_ada:
                    ada_run(l + 1, range(n * 12, n * 12 + 3))
                rcw = lambda k: P(f"rcw{e}", n, k)
                kb.act(XC[:], ps[:, 0:T], AF.Identity, [ps.R(), PAR.R()], [XC.R()], scale=rcw(2), bias=P(f"rcb{e}", n))
                for k in (0, 1, 3):
                    d = k - 2
                    for (s0, Ls) in seqs:
                        lo = s0 + max(0, -d)
                        hi = s0 + Ls - max(0, d)
                        kb.stt(XC[:, lo:hi], ps[:, lo + d:hi + d], rcw(k), XC[:, lo:hi], ALU.mult, ALU.add,
                               [ps.R(), XC.R(), PAR.R()], [XC.R()])
                kb.act(XCB[:], XC[:], AF.Copy, [XC.R()], [XCB.R()])
                gs = gring.get(("gw", n))
                gw3 = gs.t[:].rearrange("k (b j) -> k b j", b=4)
                for d in range(2):
                    A_d, BX_d = (A_, BX) if d == 0 else (A2, BX2)
                    psr = psget()
                    psi = psget()
                    for g, pg in ((0, psr), (1, psi)):
                        for t in range(T // 512):
                            kb.mm([MM(pg[:, t * 512:(t + 1) * 512], gw3[:, d * 2 + g, :],
                                      XCB[:, t * 512:(t + 1) * 512])], [gs.R(), XCB.R()], [pg.R()])
                    bg = lambda g: P(f"rbg{e}", d, g, n)
                    kb.act(A_d[:], psr[:, 0:T], AF.Sigmoid, [psr.R(), PAR.R()], [A_d.R()], bias=bg(0))
                    kb.act(BX_d[:], psi[:, 0:T], AF.Sigmoid, [psi.R(), PAR.R()], [BX_d.R()], bias=bg(1))
                    HD = HF if d == 0 else HB
                    kb.act(HD[:], A_d[:], AF.Exp, [A_d.R(), CD.R()], [HD.R()], scale=CD[:, 1, d, n:n + 1])
                    kb.act(A_d[:], A_d[:], AF.Exp, [A_d.R(), CD.R()], [A_d.R()], scale=CD[:, 0, d, n:n + 1])
                    kb.act(HD[:], HD[:], AF.Sqrt, [HD.R(), EPSV.R()], [HD.R()], scale=-1.0, bias=EPSV[:, 1:2])
                    kb.tt(BX_d[:], BX_d[:], HD[:], ALU.mult, [BX_d.R(), HD.R()], [BX_d.R()])
                    kb.tt(BX_d[:], BX_d[:], XC[:], ALU.mult, [BX_d.R(), XC.R()], [BX_d.R()])
                    for si, (s0, Ls) in enumerate(seqs):
                        init = 0.0 if pname == "p" else H0S[:, d, n:n + 1]
                        if d == 0:
                            o_, a_, b_ = HD[:, s0:s0 + Ls], A_d[:, s0:s0 + Ls], BX_d[:, s0:s0 + Ls]
                        else:
                            o_, a_, b_ = (HD[:, s0:s0 + Ls][:, ::-1], A_d[:, s0:s0 + Ls][:, ::-1],
                                          BX_d[:, s0:s0 + Ls][:, ::-1])
                        kb.op("dve", lambda e_: e_.tensor_tensor_scan(out=o_, data0=a_, data1=b_, initial=init,
                                                                      op0=ALU.mult, op1=ALU.add),
                              [A_d.R(), BX_d.R(), H0S.R()], [HD.R()])
                        if pname == "p":
                            col = s0 + Ls - 1 if d == 0 else s0
                            kb.cp(NST[:, e, si, d, n:n + 1], HD[:, col:col + 1], [HD.R()], [NST.R()])
                gring.release(gs)
                if do_ada:
                    ada_run(l + 1, range(n * 12 + 3, n * 12 + 6))
                s = ring.get(("win", pname, l, n, 1))
                ps = psget()
                proj_cols(s, Hh, T, ps)
                ring.release(s)
                kb.act(A_[:], ps[:, 0:T], AF.Gelu_apprx_tanh, [ps.R()], [A_.R()])
                kb.tt(HF[:], HF[:], HB[:], ALU.add, [HF.R(), HB.R()], [HF.R()])
                kb.tt(Y[:, n, :], HF[:], A_[:], ALU.mult, [HF.R(), A_.R()], [Y.R(n)])
                sa = ring.get(("win", pname, l, n, 2))
                psa = psget()
                proj_cols(sa, Hh, T, psa)
                ring.release(sa)
                sb_ = ring.get(("win", pname, l, n, 3))
                psb = psget()
                proj_cols(sb_, Hh, T, psb)
                ring.release(sb_)
                kb.act(HB[:], psb[:, 0:T], AF.Sigmoid, [psb.R()], [HB.R()])
                kb.tt(XCB[:], psa[:, 0:T], HB[:], ALU.mult, [psa.R(), HB.R()], [XCB.R()])
                if do_ada:
                    ada_run(l + 1, range(n * 12 + 6, n * 12 + 9))
                ccw = lambda k: P(f"ccw{e}", n, k)
                pc = psget()
                order = [15] + [k for k in range(31) if k != 15]
                for ti, k in enumerate(order):
                    d = k - 15
                    dg = DG[ti % 4]
                    kb.ts(dg[:], IDF[:], ccw(k), None, ALU.mult, None, [IDF.R(), PAR.R()], [dg.R()])
                    fns = []
                    if ti == 0:
                        for hf in range(T // 512):
                            fns.append(MM(pc[:, hf * 512:(hf + 1) * 512], dg[:], XCB[:, hf * 512:(hf + 1) * 512],
                                          True, False))
                    for (s0_, Ls) in (seqs if ti > 0 else []):
                        lo = s0_ + max(0, -d)
                        hi = s0_ + Ls - max(0, d)
                        for hf in range(T // 512):
                            a = max(lo, hf * 512)
                            b_ = min(hi, (hf + 1) * 512)
                            if b_ > a:
                                fns.append(MM(pc[:, a:b_], dg[:], XCB[:, a + d:b_ + d], ti == 0, ti == 30))
                    kb.mm(fns, [dg.R(), XCB.R()], [pc.R()])
                kb.act(Y[:, 8 + n, :], pc[:, 0:T], AF.Identity, [pc.R(), PAR.R()], [Y.R(8 + n)], bias=P(f"ccb{e}", n))
                if do_ada:
                    ada_run(l + 1, range(n * 12 + 9, n * 12 + 12))
            ps1 = psget()
            ps2 = psget()
            for n in range(8):
                for t in range(T // 512):
                    cs = slice(t * 512, (t + 1) * 512)
                    kb.mm([MM(ps1[:, cs], ONES[:], Y[:, 8 + n, cs], n == 0, n == 7)], [ONES.R(), Y.R(8 + n)], [ps1.R()])
                kb.act(XCB[:], Y[:, 8 + n, :], AF.Square, [Y.R(8 + n)], [XCB.R()])
                for t in range(T // 512):
                    cs = slice(t * 512, (t + 1) * 512)
                    kb.mm([MM(ps2[:, cs], ONES[:], XCB[:, cs], n == 0, n == 7)], [ONES.R(), XCB.R()], [ps2.R()])
            kb.act(A_[:], ps1[:, 0:T], AF.Copy, [ps1.R()], [A_.R()], scale=1.0 / 1024)
            kb.tt(HB[:], A_[:], A_[:], ALU.mult, [A_.R()], [HB.R()])
            kb.stt(BX[:], ps2[:, 0:T], 1.0 / 1024, HB[:], ALU.mult, ALU.subtract, [ps2.R(), HB.R()], [BX.R()])
            kb.act(BX[:], BX[:], AF.Sqrt, [BX.R(), EPSV.R()], [BX.R()], bias=EPSV[:, 0:1])
            kb.op("dve", lambda e_: e_.reciprocal(out=BX[:], in_=BX[:]), [BX.R()], [BX.R()])
            for n in range(8):
                kb.tt(HF[:], Y[:, 8 + n, :], A_[:], ALU.subtract, [Y.R(8 + n), A_.R()], [HF.R()])
                kb.tt(HF[:], HF[:], BX[:], ALU.mult, [HF.R(), BX.R()], [HF.R()])
                kb.act(Y[:, 8 + n, :], HF[:], AF.Silu, [HF.R(), PAR.R()], [Y.R(8 + n)],
                       scale=P(f"lng{e}", n), bias=P(f"lnb{e}", n))

        def odd_mixer(pname, l, T, seqs, Hh, Y, tes):
            o = l // 2
            NTC = T // 128
            QB = kb.sb("QB", [128, T], BF16, tes)
            KBF = kb.sb("KBF", [128, T], BF16, tes)
            VB = kb.sb("VB", [128, NTC, 128], BF16, tes)
            PBs = [kb.sb("PB", [128, NKS], BF16, tes) for _ in range(2)]
            PTs = [kb.sb("PT", [128, 7, 128], BF16, tes) for _ in range(2)]
            MXs = [kb.sb("MX", [128, 4], F32, tes) for _ in range(2)]
            RDs = [kb.sb("RD", [128, 128], F32, tes) for _ in range(2)]
            if pname == "p":
                KF = kb.sb("KF", [128, 512], F32, tes)
                VF = kb.sb("VF", [128, 4, 128], F32, tes)
            elif "units" not in SKIP:
                tring = Ring(kb, 3, [128, NKW], BF16, "tring", tes)
                tring.plan = [p_ for p_ in tplan if p_[0][1] == l]
                kring = Ring(kb, 2, [128, 256], BF16, "kring", tes)
                kring.plan = [p_ for p_ in kplan if p_[0][1] == l]
                vring = Ring(kb, 2, [128, 256], BF16, "vring", tes)
                vring.plan = [p_ for p_ in vplan if p_[0][1] == l]
            for hd in range(NH):
                s = ring.get(("wqkv", pname, l, hd, 0))
                ps = psget(2)
                proj_cols(s, Hh, T, ps)
                ring.release(s)
                kb.act(QB[:], ps[:, 0:T], AF.Copy, [ps.R()], [QB.R()], scale=float(128 ** -0.5))
                do_ada = pname == "s" and l + 1 < nl
                if do_ada:
                    ada_run(l + 1, range(hd * 6, hd * 6 + 2), 2)
                s = ring.get(("wqkv", pname, l, hd, 1))
                ps = psget(2)
                proj_cols(s, Hh, T, ps)
                ring.release(s)
                kb.act(KBF[:], ps[:, 0:T], AF.Copy, [ps.R()], [KBF.R()])
                if pname == "p" and "kvout" not in SKIP:
                    kb.act(KF[:], ps[:, 0:512], AF.Copy, [ps.R()], [KF.R()])
                    kb.dma("sp", nk_d[o, :, hd, :], KF[:], ds_out, reads=[KF.R()])
                if do_ada:
                    ada_run(l + 1, range(hd * 6 + 2, hd * 6 + 4), 2)
                s = ring.get(("wqkv", pname, l, hd, 2))
                s3 = slot3(s)
                ps = psget(2)
                for tc in range(NTC if "vproj" not in SKIP else 0):
                    kb.mm([MM(ps[:, tc * 128:(tc + 1) * 128], Hh[:, k, tc * 128:(tc + 1) * 128], s3[:, k, :],
                              k == 0, k == 15) for k in range(16)], [s.R()] + [Hh.R(k) for k in range(16)], [ps.R()])
                ring.release(s)
                kb.act(VB[:].rearrange("p c d -> p (c d)"), ps[:, 0:T], AF.Copy, [ps.R()], [VB.R()])
                if pname == "p" and "kvout" not in SKIP:
                    kb.act(VF[:].rearrange("p c d -> p (c d)"), ps[:, 0:512], AF.Copy, [ps.R()], [VF.R()])
                    kb.dma("sp", nv_d[o, :, :, hd * 128:(hd + 1) * 128], VF[:], ds_out, reads=[VF.R()])
                if do_ada:
                    ada_run(l + 1, range(hd * 6 + 4, hd * 6 + 6), 2)
                if "units" in SKIP:
                    units = []
                elif pname == "p":
                    units = [(s0 + qq * 128, s0, None) for (s0, Ls) in seqs for qq in range(Ls // 128)]
                else:
                    units = [(qt * 128, QT_R0[qt] * 64, qt) for qt in range(8)]
                    kc_ = kring.get(("kc", l, hd))
                    vc_ = vring.get(("vc", l, hd))
                    vc3 = vc_.t[:].rearrange("p (c d) -> p c d", c=2)

                def stageA(i):
                    q0, k0, qt = units[i]
                    ps = psget(2)
                    qs = QB[:, q0:q0 + 128]
                    if qt is None:
                        kb.mm([MM(ps[:, 0:256], qs, KBF[:, k0:k0 + 256])], [QB.R(), KBF.R()], [ps.R()])
                    else:
                        tb = tring.get(("bt", l, hd, qt))
                        kb.mm([MM(ps[:, 0:512], qs, KBF[:, k0:k0 + 512], True, False),
                               MM(ps[:, 0:512], IDB[:], tb.t[:, 0:512], False, True),
                               MM(ps[:, 512:640], qs, KBF[:, k0 + 512:k0 + 640], True, False),
                               MM(ps[:, 512:640], IDB[:], tb.t[:, 512:640], False, True),
                               MM(ps[:, 640:896], qs, kc_.t[:, :])],
                              [QB.R(), KBF.R(), IDB.R(), tb.R(), kc_.R()], [ps.R()])
                        tring.release(tb)
                    return ps

                def stageB(i, ps):
                    q0, k0, qt = units[i]
                    nks = 256 if qt is None else NKS
                    PB, MX = PBs[i % 2], MXs[i % 2]
                    kb.op("dve", lambda e_: e_.tensor_reduce(out=MX[:, 0:1], in_=ps[:, 0:nks], axis=AX.X, op=ALU.max),
                          [ps.R()], [MX.R()])
                    kb.ts(MX[:, 1:2], MX[:, 0:1], -1.0, None, ALU.mult, None, [MX.R()], [MX.R()])
                    kb.act(PB[:, 0:nks], ps[:, 0:nks], AF.Exp, [ps.R(), MX.R()], [PB.R()], bias=MX[:, 1:2])

                def stageC1(i):
                    q0, k0, qt = units[i]
                    nch = (256 if qt is None else NKS) // 128
                    PB, PT = PBs[i % 2], PTs[i % 2]
                    pst = PSH[4 + i % 2]
                    pstb = pst.ap.bitcast(BF16)
                    kb.mm([(lambda c: lambda pe: pe.transpose(out=pstb[:, c * 128:(c + 1) * 128],
                                                               in_=PB[:, c * 128:(c + 1) * 128], identity=IDB[:]))(c)
                           for c in range(nch)], [PB.R(), IDB.R()], [pst.R()])
                    kb.act(PT[:].rearrange("p c q -> p (c q)")[:, 0:nch * 128], pstb[:, 0:nch * 128], AF.Copy,
                           [pst.R()], [PT.R()])

                def stageC2(i):
                    q0, k0, qt = units[i]
                    nch = (256 if qt is None else NKS) // 128
                    PT = PTs[i % 2]
                    RD = RDs[i % 2]
                    pso = PSH[6 + i % 2]
                    fns = []
                    rds = [PT.R(), VB.R(), ONES.R()]
                    for c in range(nch):
                        if qt is None or c < 5:
                            vch = VB[:, k0 // 128 + c, :]
                        else:
                            vch = vc3[:, c - 5, :]
                        fns.append(MM(pso[:, 0:128], vch, PT[:, c, :], c == 0, c == nch - 1))
                    for c in range(nch):
                        fns.append(MM(pso[:, 128:256], ONES[:], PT[:, c, :], c == 0, c == nch - 1))
                    if qt is not None:
                        rds.append(vc_.R())
                    kb.mm(fns, rds, [pso.R()])
                    kb.op("dve", lambda e_: e_.reciprocal(out=RD[:], in_=pso[:, 128:256]), [pso.R()], [RD.R()])
                    kb.tt(Y[:, hd, q0:q0 + 128], pso[:, 0:128], RD[:], ALU.mult, [pso.R(), RD.R()], [Y.R(hd)])

                nu = len(units)
                for i in range(-1, nu):
                    if i + 1 < nu:
                        ps_ = stageA(i + 1)
                        stageB(i + 1, ps_)
                    if i >= 0:
                        stageC1(i)
                    if i >= 1:
                        stageC2(i - 1)
                if nu:
                    stageC2(nu - 1)
                if pname == "s" and "units" not in SKIP:
                    kring.release(kc_)
                    vring.release(vc_)

        def ffn(pname, l, T, seqs, X, cnd):
            with ExitStack() as fs:
                HFN = kb.sb("HFN", [128, KC, T], BF16, fs)
                ACC = kb.sb("ACC", [128, KC, 512], F32, fs)
                ACTB = [kb.sb(f"ACTB{i}", [128, GRP, 512], BF16, fs) for i in range(2)]
                CGV = [kb.sb("CG", [128, 512], F32, fs), kb.sb("CV", [128, 512], F32, fs)]
                nt = NormTmp(fs)
                premod(X, HFN, T, DER[:, 2, :], MODS[:, l, cnd, 48:64], [MODS.R((l, cnd))], nt, pshget)
                cc = [0]
                starts = [s0 for (s0, Ls) in seqs]
                NW = T // 512
                st = {}

                def wave_geom(w):
                    c0 = w * 512
                    segs = []
                    for (s0, Ls) in seqs:
                        a = max(s0, c0)
                        b_ = min(s0 + Ls, c0 + 512)
                        if b_ > a:
                            segs.append((a - c0, b_ - c0))
                    lh = c0 - 1 if (c0 > 0 and c0 not in starts) else None
                    rh = c0 + 512 if (c0 + 512 < T and (c0 + 512) not in starts) else None
                    return c0, segs, lh, rh

                def chunk_front(w, j):
                    c0, segs, lh, rh = wave_geom(w)
                    pgv = (PSH[2 * (cc[0] % 2)], PSH[2 * (cc[0] % 2) + 1])
                    HT = PSH[6 + cc[0] % 2]
                    cc[0] += 1
                    hcol = (hc[0] % 128) * 4
                    hc[0] += 1
                    halo = lh is not None or rh is not None
                    for part in range(2):
                        s_ = ring.get(("wup", pname, l, w, j, part))
                        s3 = slot3(s_)
                        pp = pgv[part]
                        fns = []
                        for k in range(16):
                            fns.append(MM(pp[:, 0:512], s3[:, k, :], HFN[:, k, c0:c0 + 512], k == 0, k == 15))
                            if lh is not None:
                                fns.append(MM(HT[:, hcol + part * 2:hcol + part * 2 + 1], s3[:, k, :],
                                              HFN[:, k, lh:lh + 1], k == 0, k == 15))
                            if rh is not None:
                                fns.append(MM(HT[:, hcol + part * 2 + 1:hcol + part * 2 + 2], s3[:, k, :],
                                              HFN[:, k, rh:rh + 1], k == 0, k == 15))
                        kb.mm(fns, [s_.R()] + [HFN.R(k) for k in range(16)], [pp.R()] + ([HT.R()] if halo else []))
                        ring.release(s_)
                    st[(w, j)] = (pgv, HT, hcol)

                def chunk_back(w, j, AB, jj):
                    c0, segs, lh, rh = wave_geom(w)
                    pgv, HT, hcol = st.pop((w, j))
                    for part in range(2):
                        pp = pgv[part]
                        CB = CGV[part]
                        c = part * NFF + j
                        u = pp[:, 0:512]
                        w_ = lambda k: P(f"fcw{l}", c, k)
                        kb.act(CB[:], u, AF.Identity, [pp.R(), PAR.R()], [CB.R()], scale=w_(1), bias=P(f"fcb{l}", c))
                        for (a, b_) in segs:
                            if b_ - a > 1:
                                kb.stt(CB[:, a + 1:b_], u[:, a:b_ - 1], w_(0), CB[:, a + 1:b_], ALU.mult, ALU.add,
                                       [pp.R(), CB.R(), PAR.R()], [CB.R()])
                                kb.stt(CB[:, a:b_ - 1], u[:, a + 1:b_], w_(2), CB[:, a:b_ - 1], ALU.mult, ALU.add,
                                       [pp.R(), CB.R(), PAR.R()], [CB.R()])
                        if lh is not None:
                            kb.stt(CB[:, 0:1], HT[:, hcol + part * 2:hcol + part * 2 + 1], w_(0), CB[:, 0:1],
                                   ALU.mult, ALU.add, [HT.R(), CB.R(), PAR.R()], [CB.R()])
                        if rh is not None:
                            kb.stt(CB[:, 511:512], HT[:, hcol + part * 2 + 1:hcol + part * 2 + 2], w_(2),
                                   CB[:, 511:512], ALU.mult, ALU.add, [HT.R(), CB.R(), PAR.R()], [CB.R()])
                    kb.act(CGV[0][:], CGV[0][:], AF.Gelu_apprx_tanh, [CGV[0].R()], [CGV[0].R()])
                    kb.tt(AB[:, jj, :], CGV[0][:], CGV[1][:], ALU.mult, [CGV[0].R(), CGV[1].R()], [AB.R(jj)])

                for w in range(NW):
                    for g in range(NFF // GRP):
                        AB = ACTB[g % 2]
                        for jj in range(GRP):
                            j = g * GRP + jj
                            if (w, j) not in st:
                                chunk_front(w, j)
                            chunk_back(w, j, AB, jj)
                        sds = [ring.get(("wdn", pname, l, w, g * GRP + jj)) for jj in range(GRP)]
                        for m in range(16):
                            ps = pshget()
                            kb.mm([MM(ps[:, 0:512], sds[jj].t[:, m * 128:(m + 1) * 128], AB[:, jj, :], jj == 0,
                                      jj == GRP - 1) for jj in range(GRP)],
                                  [sd.R() for sd in sds] + [AB.R(jj) for jj in range(GRP)], [ps.R()])
                            if g == 0:
                                kb.cp(ACC[:, m, :], ps[:, 0:512], [ps.R()], [ACC.R(m)])
                            else:
                                kb.tt(ACC[:, m, :], ACC[:, m, :], ps[:, 0:512], ALU.add, [ACC.R(m), ps.R()], [ACC.R(m)])
                        for sd in sds:
                            ring.release(sd)
                    if w + 1 < NW:
                        chunk_front(w + 1, 0)
                        chunk_front(w + 1, 1)
                    post_residual(ACC, X, slice(w * 512, (w + 1) * 512), DER[:, 3, :], nt, pshget)
                kb.barrier()

        for pname, T, seqs in passes:
            cnd = 0 if pname == "p" else 1
            with ExitStack() as pes:
                X = kb.sb("X" + pname, [128, KC, T], F32, pes)
                xsrc = xp_d if pname == "p" else xs_d
                for k in range(KC):
                    kb.dma("sp", X[:, k, :], xsrc[:, k, :], ds_x, writes=[X.R(k)])
                for k in range(KC):
                    X.R(k).w = ("d", ds_x, ds_x["c"])
                for l in range(nl):
                    derive(l, cnd)
                    with ExitStack() as ys:
                        Y = kb.sb("Y", [128, KC, T], BF16, ys)
                        with ExitStack() as hs:
                            Hh = kb.sb("H", [128, KC, T], BF16, hs)
                            with ExitStack() as ns:
                                nt = NormTmp(ns)
                                premod(X, Hh, T, DER[:, 0, :], MODS[:, l, cnd, 0:16], [MODS.R((l, cnd))], nt)
                                kb.barrier()
                            with ExitStack() as tes:
                                if l % 2 == 0:
                                    even_mixer(pname, l, T, seqs, Hh, Y, tes)
                                else:
                                    odd_mixer(pname, l, T, seqs, Hh, Y, tes)
                                kb.barrier()
                        with ExitStack() as os_:
                            OB = kb.sb("OB", [128, KC, 512], F32, os_)
                            nt = NormTmp(os_)
                            outproj_residual(pname, l, T, X, Y, OB, nt)
                            kb.barrier()
                    ffn(pname, l, T, seqs, X, cnd)
                ydst = yp_d if pname == "p" else ys_d
                for k in range(KC):
                    kb.dma("sp", ydst[:, k, :], X[:, k, :], ds_out, reads=[X.R(k)])
                if pname == "p":
                    for e in range(2):
                        kb.dma("sp", nst_d[e], NST[:, e].rearrange("p s d n -> p (s d n)"), ds_out, reads=[NST.R()])
                kb.barrier()
        kb.barrier()
    return nc


_PROG = {}


def _btab(rel):
    tab = np.full((2, NH, 5, 128, NKW), -1e30, np.float32)
    for vi, qt in enumerate([0, 1, 2, 6, 7]):
        r0, R0 = 2 * qt, QT_R0[qt]
        q = np.arange(128)
        r = r0 + q // 64
        c = q % 64
        kk = np.arange(NKW)
        kr = R0 + kk // 64
        kc = kk % 64
        rsr = np.clip(r - 4, 0, 8)
        csc = np.clip(c - 8, 0, 48)
        valid = ((kr[None, :] >= rsr[:, None]) & (kr[None, :] < rsr[:, None] + 8) &
                 (kc[None, :] >= csc[:, None]) & (kc[None, :] < csc[:, None] + 16))
        dr = np.clip(kr[None, :] - r[:, None] + 7, 0, 14)
        dc = np.clip(kc[None, :] - c[:, None], -15, 15) + 15
        vals = rel[:, :, dr, dc]
        tab[:, :, vi] = np.where(valid[None, None], vals, np.float32(-1e30))
    return tab


def kernel(**inp):
    nl = int(os.environ.get("MK_NL", L))
    g = {k: np.asarray(v) for k, v in inp.items()}
    poff, NP = param_layout()
    par = np.zeros((128, NP), np.float32)

    def put(name, arr):
        o, shape = poff[name]
        par[:, o:o + int(np.prod(shape))] = np.ascontiguousarray(arr).reshape(128, -1)
    for l in range(L):
        put(f"b_ada{l}", g["b_ada"][l].reshape(96, 128).T)
        put(f"norm_g{l}", g["norm_g"][l].reshape(4, 16, 128).transpose(2, 0, 1))
        put(f"fcw{l}", g["ffn_conv_w"][l].reshape(3, 88, 128).transpose(2, 1, 0))
        put(f"fcb{l}", g["ffn_conv_b"][l].reshape(88, 128).T)
    for e in range(2):
        put(f"rcw{e}", g["rg_conv_w"][e].reshape(4, 8, 128).transpose(2, 1, 0))
        put(f"rcb{e}", g["rg_conv_b"][e].reshape(8, 128).T)
        put(f"rbg{e}", g["rg_b_gate"][e].reshape(2, 2, 8, 128).transpose(3, 0, 1, 2))
        put(f"lam{e}", g["rg_lambda"][e].reshape(2, 8, 128).transpose(2, 0, 1))
        put(f"ccw{e}", g["cf_conv_w"][e].reshape(31, 8, 128).transpose(2, 1, 0))
        put(f"ccb{e}", g["cf_conv_b"][e].reshape(8, 128).T)
        put(f"lng{e}", g["cf_ln_g"][e].reshape(8, 128).T)
        put(f"lnb{e}", g["cf_ln_b"][e].reshape(8, 128).T)
    btab = _btab(g["na_rel_bias"])
    ident = np.eye(128, dtype=np.float32)
    shared = {"par": par, "btab": btab, "ident": ident}
    def tile_cols(W):
        lead = W.shape[:-2]
        N = W.shape[-1]
        nl_ = len(lead)
        Wt = W.reshape(*lead, 16, 128, N // 128, 128)
        perm = tuple(range(nl_)) + (nl_ + 2, nl_ + 1, nl_ + 0, nl_ + 3)
        return np.ascontiguousarray(Wt.transpose(perm)).reshape(*lead, N // 128, 128, 16 * 128)
    for k in ("w_ada", "w_in_even", "w_out_even", "w_qkv_odd", "w_out_odd", "w_ffn_up"):
        shared[k] = tile_cols(np.asarray(g[k], dtype=np.float32))
    for k in ("rg_w_gate", "w_ffn_down"):
        shared[k] = np.ascontiguousarray(g[k], dtype=np.float32)

    def fm(x):
        return np.ascontiguousarray(x.reshape(x.shape[0], KC, 128).transpose(2, 1, 0))
    in_maps = []
    for i in range(NCORES):
        m = dict(shared)
        m["xp"] = fm(g["x_prompt"][2 * i:2 * i + 2].reshape(512, D))
        m["xs"] = fm(g["x_sample"][i])
        cond = np.stack([g["c_ctx"], g["c"][i]], axis=-1)
        m["cond"] = np.ascontiguousarray(cond.reshape(KC, 128, 2).transpose(1, 0, 2))
        m["kctx"] = np.ascontiguousarray(g["cache_k"][i].transpose(0, 3, 2, 1))
        m["vctx"] = np.ascontiguousarray(g["cache_v"][i].reshape(2, 2, 128, D).transpose(0, 2, 1, 3))
        m["h0"] = np.ascontiguousarray(g["state_rglru"][i].reshape(2, 2, 8, 128).transpose(0, 3, 1, 2))
        in_maps.append(m)
    key = nl
    if key not in _PROG:
        mp = os.environ.get("MK_PASS", "ps")
        _PROG[key] = build_program(nl, do_sample="s" in mp, do_prompt="p" in mp)
    ncr = int(os.environ.get("MK_CORES", NCORES))
    res = run_bass_kernel_spmd(_PROG[key], in_maps[:ncr], core_ids=list(range(ncr)))
    yp = np.zeros((16, 256, D), np.float32)
    ys = np.zeros((8, 1024, D), np.float32)
    nk = np.zeros((16, 2, 256, NH, 128), np.float32)
    nv = np.zeros((16, 2, 256, NH, 128), np.float32)
    nst = np.zeros((16, 2, 2, 1024), np.float32)
    for i, r in enumerate(res.results):
        yp[2 * i:2 * i + 2] = r["yp"].transpose(2, 1, 0).reshape(2, 256, D)
        ys[i] = r["ys"].transpose(2, 1, 0).reshape(1024, D)
        k_ = r["nk"].transpose(0, 3, 2, 1).reshape(2, 2, 256, NH, 128)
        nk[2 * i:2 * i + 2] = k_.transpose(1, 0, 2, 3, 4)
        v_ = r["nv"].transpose(0, 2, 1, 3).reshape(2, 2, 256, NH, 128)
        nv[2 * i:2 * i + 2] = v_.transpose(1, 0, 2, 3, 4)
        s_ = r["nst"].transpose(2, 0, 3, 4, 1).reshape(2, 2, 2, 1024)
        nst[2 * i:2 * i + 2] = s_
    return yp, ys, nk, nv, nst
```

```python
import os
from contextlib import ExitStack
import numpy as np
import concourse.bass as bass
import concourse.mybir as mybir
from concourse.bass_utils import run_bass_kernel_spmd

F32, BF16 = mybir.dt.float32, mybir.dt.bfloat16
AF = mybir.ActivationFunctionType
ALU = mybir.AluOpType
AX = mybir.AxisListType

NCORES = 8
L = 4
D = 2048
KC = 16
DFF = 5632
NFF = 44
GRP = 4
NH = 16
EPS = 1e-6
NRING = 7
WIN_ROWS = 10
NKW = WIN_ROWS * 64
NKS = NKW + 256
QT_R0 = [0, 0, 0, 2, 4, 6, 6, 6]
QT_VAR = [0, 1, 2, 2, 2, 2, 3, 4]


def param_layout():
    off = {}
    n = 0

    def add(name, *shape):
        nonlocal n
        off[name] = (n, shape)
        n += int(np.prod(shape))
    for l in range(L):
        add(f"b_ada{l}", 96)
        add(f"norm_g{l}", 4, 16)
        add(f"fcw{l}", 88, 3)
        add(f"fcb{l}", 88)
    for e in range(2):
        add(f"rcw{e}", 8, 4)
        add(f"rcb{e}", 8)
        add(f"rbg{e}", 2, 2, 8)
        add(f"lam{e}", 2, 8)
        add(f"ccw{e}", 8, 31)
        add(f"ccb{e}", 8)
        add(f"lng{e}", 8)
        add(f"lnb{e}", 8)
    return off, n


class Res:
    __slots__ = ("w", "r", "excl")

    def __init__(self, excl=False):
        self.w = None
        self.r = []
        self.excl = excl


class TB:
    def __init__(self, t, excl=False):
        self.t = t
        self.res = {}
        self.excl = excl

    def R(self, key=0):
        r = self.res.get(key)
        if r is None:
            r = self.res[key] = Res(self.excl)
        return r

    def __getitem__(self, k):
        return self.t[k]


class KB:
    def __init__(self, nc, es):
        self.nc = nc
        self.es = es
        self.eng = {"pe": nc.tensor, "act": nc.scalar, "dve": nc.vector, "pool": nc.gpsimd, "sp": nc.sync}
        self.sem = {e: es.enter_context(nc.semaphore("s_" + e)) for e in self.eng}
        self.cnt = {e: 0 for e in self.eng}
        self.seen = {e: {} for e in self.eng}
        self.dsems = []
        self.nd = 0

    def sb(self, name, shape, dt, es=None):
        self.nd += 1
        return TB((es or self.es).enter_context(self.nc.sbuf_tensor(f"{name}_{self.nd}", list(shape), dt)))

    def dsem(self, name, ring=False):
        self.nd += 1
        s = self.es.enter_context(self.nc.semaphore(f"{name}_{self.nd}"))
        h = {"s": s, "c": 0, "id": len(self.dsems), "ring": ring}
        self.dsems.append(h)
        return h

    def _wait(self, eng, dep):
        if dep[0] == "e":
            key = ("e", dep[1])
            semh = self.sem[dep[1]]
        else:
            key = ("d", dep[1]["id"])
            semh = dep[1]["s"]
        val = dep[2]
        if self.seen[eng].get(key, 0) >= val:
            return
        self.seen[eng][key] = val
        self.eng[eng].wait_ge(semh, val)

    def _deps(self, eng, reads, writes):
        for r in reads:
            if r.w is not None:
                self._wait_f(eng, r.w)
            if r.excl:
                for d in r.r:
                    if not (d[0] == "e" and d[1] == eng):
                        self._wait_f(eng, d)
        for w in writes:
            if w.w is not None:
                self._wait_f(eng, w.w)
            for d in w.r:
                self._wait_f(eng, d)

    def _wait_f(self, eng, d):
        if eng == "pe" and d[0] == "e" and d[1] == "pe":
            return
        self._wait(eng, d)

    def _mark(self, me, reads, writes):
        for w in writes:
            w.w = me
            w.r = []
        for r in reads:
            if r in writes:
                continue
            r.r.append(me)
            if len(r.r) > 24:
                last = {}
                for d in r.r:
                    k = (d[0], d[1] if d[0] == "e" else d[1]["id"])
                    if k not in last or last[k][2] < d[2]:
                        last[k] = d
                r.r = list(last.values())

    def op(self, eng, fn, reads=(), writes=()):
        self._deps(eng, reads, writes)
        ins = fn(self.eng[eng])
        ins.then_inc(self.sem[eng], 1)
        self.cnt[eng] += 1
        self._mark(("e", eng, self.cnt[eng]), reads, writes)

    def mm(self, fns, reads=(), writes=()):
        self._deps("pe", reads, writes)
        ins = None
        for f in fns:
            ins = f(self.nc.tensor)
        ins.then_inc(self.sem["pe"], 1)
        self.cnt["pe"] += 1
        self._mark(("e", "pe", self.cnt["pe"]), reads, writes)

    def dma(self, q, out, in_, ds, reads=(), writes=()):
        self._deps(q, reads, writes)
        self.eng[q].dma_start(out=out, in_=in_).then_inc(ds["s"], 16)
        ds["c"] += 16
        self._mark(("d", ds, ds["c"]), reads, writes)

    def barrier(self, full=False):
        for e in self.eng:
            if e == "pool" and not full:
                continue
            for e2 in self.eng:
                if e2 != e and self.cnt[e2] > 0 and (full or e2 != "pool"):
                    self._wait(e, ("e", e2, self.cnt[e2]))
            for ds in self.dsems:
                if ds["c"] > 0 and (full or not ds["ring"]):
                    self._wait(e, ("d", ds, ds["c"]))

    def act(self, out, in_, func, reads, writes, **kw):
        self.op("act", lambda e: e.activation(out=out, in_=in_, func=func, **kw), reads, writes)

    def tt(self, out, a, b, op, reads, writes, eng="dve"):
        self.op(eng, lambda e: e.tensor_tensor(out=out, in0=a, in1=b, op=op), reads, writes)

    def stt(self, out, in0, scalar, in1, op0, op1, reads, writes, eng="dve"):
        self.op(eng, lambda e: e.scalar_tensor_tensor(out=out, in0=in0, scalar=scalar, in1=in1, op0=op0, op1=op1),
                reads, writes)

    def ts(self, out, in0, s1, s2, op0, op1, reads, writes, eng="dve"):
        if s2 is None:
            self.op(eng, lambda e: e.tensor_scalar(out=out, in0=in0, scalar1=s1, scalar2=None, op0=op0), reads, writes)
        else:
            self.op(eng, lambda e: e.tensor_scalar(out=out, in0=in0, scalar1=s1, scalar2=s2, op0=op0, op1=op1),
                    reads, writes)

    def cp(self, out, in_, reads, writes, eng="dve"):
        self.op(eng, lambda e: e.tensor_copy(out=out, in_=in_), reads, writes)


class Ring:
    def __init__(self, kb, n, shape, dt, name, es=None):
        self.kb = kb
        self.n = n
        self.slots = [kb.sb(f"{name}{i}", shape, dt, es) for i in range(n)]
        self.ds = [kb.dsem(f"{name}_d{i}", ring=True) for i in range(n)]
        self.held = [False] * n
        self.plan = []
        self.issued = 0
        self.consumed = 0

    def _pump(self):
        while self.issued < len(self.plan):
            i = self.issued
            sl = i % self.n
            if self.held[sl]:
                break
            s = self.slots[sl]
            out, in_ = self.plan[i][1](s)
            self.kb.dma("pool", out, in_, self.ds[sl], writes=[s.R()])
            self.held[sl] = True
            self.issued += 1

    def get(self, tag):
        c = self.consumed
        assert self.plan[c][0] == tag, (c, self.plan[c][0], tag)
        self._pump()
        assert self.issued > c, ("ring starved", tag)
        self.consumed += 1
        return self.slots[c % self.n]

    def release(self, s):
        self.held[self.slots.index(s)] = False
        self._pump()


def build_program(nl=L, do_sample=True, do_prompt=True):
    nc = bass.Bass("TRN2", target_bir_lowering=False)
    poff, NP = param_layout()

    def din(name, shape):
        return nc.dram_tensor(name, list(shape), F32, kind="ExternalInput").ap()

    def dout(name, shape):
        return nc.dram_tensor(name, list(shape), F32, kind="ExternalOutput").ap()

    xp_d = din("xp", [128, KC, 512])
    xs_d = din("xs", [128, KC, 1024])
    cond_d = din("cond", [128, KC, 2])
    kctx_d = din("kctx", [2, 128, NH, 256])
    vctx_d = din("vctx", [2, 128, 2, D])
    h0_d = din("h0", [2, 128, 2, 8])
    par_d = din("par", [128, NP])
    btab_d = din("btab", [2, NH, 5, 128, NKW])
    w_ada = din("w_ada", [L, 96, 128, 16 * 128])
    w_in = din("w_in_even", [2, 32, 128, 16 * 128])
    w_gate = din("rg_w_gate", [2, 2, 2, 8, 128, 128])
    w_oute = din("w_out_even", [2, 16, 128, 16 * 128])
    w_qkv = din("w_qkv_odd", [2, 48, 128, 16 * 128])
    w_outo = din("w_out_odd", [2, 16, 128, 16 * 128])
    w_up = din("w_ffn_up", [L, 88, 128, 16 * 128])
    w_down = din("w_ffn_down", [L, DFF, D])

    yp_d = dout("yp", [128, KC, 512])
    ys_d = dout("ys", [128, KC, 1024])
    nk_d = dout("nk", [2, 128, NH, 512])
    nv_d = dout("nv", [2, 128, 4, D])
    nst_d = dout("nst", [2, 128, 2, 2, 8])

    es = ExitStack()
    with es:
        kb = KB(nc, es)
        PAR = kb.sb("PAR", [128, NP], F32)
        MODS = kb.sb("MODS", [128, L, 2, 96], F32)
        DER = kb.sb("DER", [128, 4, 16], F32)
        CD = kb.sb("CD", [128, 2, 2, 8], F32)
        CONDF = kb.sb("CONDF", [128, KC, 2], F32)
        CONDB = kb.sb("CONDB", [128, KC, 2], BF16)
        ONES = kb.sb("ONES", [128, 128], BF16)
        IDB = kb.sb("IDB", [128, 128], BF16)
        IDF = kb.sb("IDF", [128, 128], F32)
        NST = kb.sb("NST", [128, 2, 2, 2, 8], F32)
        ring = Ring(kb, NRING, [128, 2048], BF16, "ring")
        tplan = []
        PS = [TB(es.enter_context(nc.psum_tensor(f"ps{i}", [128, 1024], F32)), excl=True) for i in range(4)]
        ps_i = [0]

        def psget(n=4):
            t = PS[ps_i[0] % n]
            ps_i[0] += 1
            return t

        ds_misc = kb.dsem("misc")
        ds_gw = kb.dsem("gw")
        ds_x = kb.dsem("x")
        ds_out = kb.dsem("out")
        ds_ctx = kb.dsem("ctx")

        def P(name, *idx):
            o, shape = poff[name]
            strides = [int(np.prod(shape[i + 1:])) for i in range(len(shape))]
            for i, s in zip(idx, strides):
                o += i * s
            rem = int(np.prod(shape[len(idx):])) if len(idx) < len(shape) else 1
            return PAR[:, o:o + rem]

        def colblk(w3d, c0):
            src = w3d[c0 // 128]
            return lambda s: (s.t[:], src)

        def rowblk(w2d, r0):
            src = w2d[r0:r0 + 128, :]
            return lambda s: (s.t[:], src)

        plan = ring.plan
        ada_pre = [0] if do_sample else list(range(nl))
        pre_cbs = range(48) if do_sample else range(96)
        for l in ada_pre:
            for cb in pre_cbs:
                plan.append((("ada", l, cb), colblk(w_ada[l], cb * 128)))
        passes = ([("s", 1024, [(0, 1024)])] if do_sample else []) + \
                 ([("p", 512, [(0, 256), (256, 256)])] if do_prompt else [])

        def plan_ada(l2, cbs):
            for cb in cbs:
                plan.append((("ada", l2, cb), colblk(w_ada[l2], cb * 128)))
        for pname, T, seqs in passes:
            for l in range(nl):
                if l % 2 == 0:
                    e = l // 2
                    for n in range(8):
                        do_ada = pname == "s" and l + 1 < nl
                        plan.append((("win", pname, l, n, 0), colblk(w_in[e], 0 * 1024 + n * 128)))
                        if do_ada:
                            plan_ada(l + 1, range(n * 12, n * 12 + 6))
                        if pname == "s" and l == 0:
                            plan_ada(0, range(48 + n * 6, 48 + n * 6 + 6))
                        for part in range(1, 4):
                            plan.append((("win", pname, l, n, part), colblk(w_in[e], part * 1024 + n * 128)))
                        if do_ada:
                            plan_ada(l + 1, range(n * 12 + 6, n * 12 + 12))
                    for w in range(T // 512):
                        for m in range(16):
                            plan.append((("wout", pname, l, w, m), colblk(w_oute[e], m * 128)))
                else:
                    o = l // 2
                    for hd in range(NH):
                        for part in range(3):
                            plan.append((("wqkv", pname, l, hd, part), colblk(w_qkv[o], part * D + hd * 128)))
                            if pname == "s" and l + 1 < nl:
                                plan_ada(l + 1, range(hd * 6 + part * 2, hd * 6 + part * 2 + 2))
                        if pname == "s":
                            for qt in range(8):
                                src = btab_d[o, hd, QT_VAR[qt]]
                                tplan.append((("bt", l, hd, qt), (lambda src: lambda s: (s.t[:], src))(src)))
                    for w in range(T // 512):
                        for m in range(16):
                            plan.append((("wout", pname, l, w, m), colblk(w_outo[o], m * 128)))
                for w in range(T // 512):
                    for g in range(NFF // GRP):
                        for jj in range(GRP):
                            j = g * GRP + jj
                            plan.append((("wup", pname, l, w, j, 0), colblk(w_up[l], j * 128)))
                            plan.append((("wup", pname, l, w, j, 1), colblk(w_up[l], DFF + j * 128)))
                        for jj in range(GRP):
                            j = g * GRP + jj
                            plan.append((("wdn", pname, l, w, j), rowblk(w_down[l], j * 128)))

        def slot3(s):
            return s.t[:].rearrange("p (k n) -> p k n", k=16)

        SKIP = os.environ.get("MK_SKIP", "").split(",")
        ident_d = din("ident", [128, 128])
        EPSV = kb.sb("EPSV", [128, 2], F32)
        H0S = kb.sb("H0S", [128, 2, 8], F32)
        kplan, vplan = [], []
        if "units" in SKIP:
            tplan.clear()
        if do_sample and "units" not in SKIP:
            for l in range(nl):
                if l % 2 == 1:
                    o = l // 2
                    for hd in range(NH):
                        ksrc = kctx_d[o, :, hd, :]
                        vsrc = vctx_d[o, :, :, hd * 128:(hd + 1) * 128]
                        kplan.append((("kc", l, hd), (lambda src: lambda s: (s.t[:], src))(ksrc)))
                        vplan.append((("vc", l, hd), (lambda src: lambda s: (
                            s.t[:].rearrange("p (c d) -> p c d", c=2), src))(vsrc)))
        hc = [0]

        class PV:
            def __init__(self, tb, half):
                self.ap = tb.t[:, half * 512:(half + 1) * 512]
                self.res = Res(excl=True)

            def R(self):
                return self.res

            def __getitem__(self, k):
                return self.ap[k]
        PSH = [PV(PS[i // 2], i % 2) for i in range(8)]
        psh_i = [0]

        def pshget():
            t = PSH[4 + psh_i[0] % 2]
            psh_i[0] += 1
            return t

        def MM(out, lhsT, rhs, start=True, stop=True):
            return lambda pe: pe.matmul(out, lhsT=lhsT, rhs=rhs, start=start, stop=stop)

        kb.dma("sp", PAR[:], par_d, ds_misc, writes=[PAR.R()])
        kb.dma("sp", CONDF[:], cond_d, ds_misc, writes=[CONDF.R()])
        kb.dma("sp", IDF[:], ident_d, ds_misc, writes=[IDF.R()])
        for t_ in (PAR, CONDF, IDF):
            t_.R().w = ("d", ds_misc, ds_misc["c"])
        kb.op("dve", lambda e: e.memset(ONES[:], 1.0), writes=[ONES.R()])
        kb.op("dve", lambda e: e.memset(EPSV[:, 0:1], EPS), writes=[EPSV.R()])
        kb.op("dve", lambda e: e.memset(EPSV[:, 1:2], 1.0), writes=[EPSV.R()])
        kb.op("dve", lambda e: e.memset(NST[:], 0.0), writes=[NST.R()])
        kb.cp(IDB[:], IDF[:], [IDF.R()], [IDB.R()])
        kb.act(CONDB[:], CONDF[:], AF.Silu, [CONDF.R()], [CONDB.R()])

        def ada_run(l2, cbs, nps=4):
            for cb in cbs:
                s = ring.get(("ada", l2, cb))
                s3 = slot3(s)
                ps = psget(nps)
                kb.mm([MM(ps[:, 0:2], s3[:, k, :], CONDB[:, k, :], k == 0, k == 15) for k in range(16)],
                      [s.R(), CONDB.R()], [ps.R()])
                ring.release(s)
                kb.ts(MODS[:, l2, :, cb], ps[:, 0:2], P(f"b_ada{l2}")[:, cb:cb + 1], None, ALU.add, None,
                      [ps.R(), PAR.R()], [MODS.R((l2, 0)), MODS.R((l2, 1))])

        for l in ada_pre:
            ada_run(l, pre_cbs)

        def derive(l, cnd):
            md = lambda m: MODS[:, l, cnd, m * 16:(m + 1) * 16]
            rd = [MODS.R((l, cnd)), PAR.R()]
            kb.stt(DER[:, 0, :], md(1), 1.0, P(f"norm_g{l}", 0), ALU.add, ALU.mult, rd, [DER.R()])
            kb.tt(DER[:, 1, :], md(2), P(f"norm_g{l}", 1), ALU.mult, rd, [DER.R()])

        def derive_ffn(l, cnd):
            md = lambda m: MODS[:, l, cnd, m * 16:(m + 1) * 16]
            rd = [MODS.R((l, cnd)), PAR.R()]
            kb.stt(DER[:, 2, :], md(4), 1.0, P(f"norm_g{l}", 2), ALU.add, ALU.mult, rd, [DER.R()])
            kb.tt(DER[:, 3, :], md(5), P(f"norm_g{l}", 3), ALU.mult, rd, [DER.R()])

        class NormTmp:
            def __init__(self, es_):
                self.RS = kb.sb("RS", [128, 512], F32, es_)
                self.SQ = [kb.sb(f"SQ{i}", [128, 512], BF16, es_) for i in range(4)]
                self.TM = [kb.sb(f"TM{i}", [128, 512], F32, es_) for i in range(2)]

        def rstd_of(getsrc, srcres, nt, getps=None):
            ps = (getps or psget)()
            for k in range(KC):
                sq = nt.SQ[k % 4]
                if k % 2 == 0:
                    kb.act(sq[:], getsrc(k), AF.Square, [srcres(k)], [sq.R()])
                else:
                    kb.tt(sq[:], getsrc(k), getsrc(k), ALU.mult, [srcres(k)], [sq.R()])
                kb.mm([MM(ps[:, 0:512], ONES[:], sq[:], k == 0, k == KC - 1)], [ONES.R(), sq.R()], [ps.R()])
            kb.act(nt.RS[:], ps[:, 0:512], AF.Ln, [ps.R(), EPSV.R()], [nt.RS.R()], scale=1.0 / D, bias=EPSV[:, 0:1])
            kb.act(nt.RS[:], nt.RS[:], AF.Exp, [nt.RS.R()], [nt.RS.R()], scale=-0.5)

        def premod(X, Hh, T, Avec, Bvec, extra_reads, nt, getps=None):
            for t in range(T // 512):
                cs = slice(t * 512, (t + 1) * 512)
                rstd_of(lambda k: X[:, k, cs], lambda k: X.R(k), nt, getps)
                for k in range(KC):
                    tm = nt.TM[k % 2]
                    kb.tt(tm[:], X[:, k, cs], nt.RS[:], ALU.mult, [X.R(k), nt.RS.R()], [tm.R()])
                    kb.act(Hh[:, k, cs], tm[:], AF.Identity, [tm.R(), DER.R()] + extra_reads, [Hh.R(k)],
                           scale=Avec[:, k:k + 1], bias=Bvec[:, k:k + 1])

        def post_residual(OB, X, cs, Gvec, nt, getps=None):
            rstd_of(lambda k: OB[:, k, :], lambda k: OB.R(k), nt, getps)
            for k in range(KC):
                tm = nt.TM[k % 2]
                kb.tt(tm[:], OB[:, k, :], nt.RS[:], ALU.mult, [OB.R(k), nt.RS.R()], [tm.R()])
                kb.stt(X[:, k, cs], tm[:], Gvec[:, k:k + 1], X[:, k, cs], ALU.mult, ALU.add,
                       [tm.R(), X.R(k), DER.R()], [X.R(k)])

        def proj_cols(s, Hh, T, ps):
            s3 = slot3(s)
            for t in range(T // 512):
                kb.mm([MM(ps[:, t * 512:(t + 1) * 512], s3[:, k, :], Hh[:, k, t * 512:(t + 1) * 512], k == 0, k == 15)
                       for k in range(16)], [s.R()] + [Hh.R(k) for k in range(16)], [ps.R()])

        def outproj_residual(pname, l, T, X, Y, OB, nt):
            for w in range(T // 512):
                cs = slice(w * 512, (w + 1) * 512)
                for m in range(16):
                    s = ring.get(("wout", pname, l, w, m))
                    s3 = slot3(s)
                    ps = psget()
                    kb.mm([MM(ps[:, 0:512], s3[:, k, :], Y[:, k, cs], k == 0, k == 15) for k in range(16)],
                          [s.R()] + [Y.R(k) for k in range(16)], [ps.R()])
                    ring.release(s)
                    kb.act(OB[:, m, :], ps[:, 0:512], AF.Copy, [ps.R()], [OB.R(m)])
                post_residual(OB, X, cs, DER[:, 1, :], nt)

        def even_mixer(pname, l, T, seqs, Hh, Y, tes):
            e = l // 2
            XC = kb.sb("XC", [128, T], F32, tes)
            XCB = kb.sb("XCB", [128, T], BF16, tes)
            HF = kb.sb("HF", [128, T], F32, tes)
            HB = kb.sb("HB", [128, T], F32, tes)
            A_ = kb.sb("A_", [128, T], F32, tes)
            BX = kb.sb("BX", [128, T], F32, tes)
            A2 = kb.sb("A2", [128, T], F32, tes)
            BX2 = kb.sb("BX2", [128, T], F32, tes)
            DG = [kb.sb("DG", [128, 128], BF16, tes) for _ in range(4)]
            gring = Ring(kb, 2, [128, 512], BF16, "gring", tes)
            for n_ in range(8):
                gsrc = w_gate[e][:, :, n_].rearrange("d g k j -> k (d g) j")
                gring.plan.append((("gw", n_), (lambda src: lambda s: (
                    s.t[:].rearrange("k (b j) -> k b j", b=4), src))(gsrc)))
            if pname == "s":
                kb.dma("sp", H0S[:], h0_d[e], ds_ctx, writes=[H0S.R()])
            lam = P(f"lam{e}")
            CDf = CD[:].rearrange("p a d n -> p a (d n)")
            kb.act(CDf[:, 0, :], lam, AF.Exp, [PAR.R()], [CD.R()], scale=-1.0)
            kb.act(CDf[:, 0, :], CDf[:, 0, :], AF.Ln, [CD.R(), EPSV.R()], [CD.R()], bias=EPSV[:, 1:2])
            kb.ts(CDf[:, 1, :], CDf[:, 0, :], -16.0, None, ALU.mult, None, [CD.R()], [CD.R()])
            kb.ts(CDf[:, 0, :], CDf[:, 0, :], -8.0, None, ALU.mult, None, [CD.R()], [CD.R()])
            for n in range(8):
                s = ring.get(("win", pname, l, n, 0))
                ps = psget()
                proj_cols(s, Hh, T, ps)
                ring.release(s)
                do_ada = pname == "s" and l + 1 < nl
                if do_ada:
                    ada_run(l + 1, range(n * 12, n * 12 + 3))
                rcw = lambda k: P(f"rcw{e}", n, k)
                kb.act(XC[:], ps[:, 0:T], AF.Identity, [ps.R(), PAR.R()], [XC.R()], scale=rcw(2), bias=P(f"rcb{e}", n))
                for k in (0, 1, 3):
                    d = k - 2
                    for (s0, Ls) in seqs:
                        lo = s0 + max(0, -d)
                        hi = s0 + Ls - max(0, d)
                        kb.stt(XC[:, lo:hi], ps[:, lo + d:hi + d], rcw(k), XC[:, lo:hi], ALU.mult, ALU.add,
                               [ps.R(), XC.R(), PAR.R()], [XC.R()])
                kb.act(XCB[:], XC[:], AF.Copy, [XC.R()], [XCB.R()])
                gs = gring.get(("gw", n))
                gw3 = gs.t[:].rearrange("k (b j) -> k b j", b=4)
                for d in range(2):
                    A_d, BX_d = (A_, BX) if d == 0 else (A2, BX2)
                    psr = psget()
                    psi = psget()
                    for g, pg in ((0, psr), (1, psi)):
                        for t in range(T // 512):
                            kb.mm([MM(pg[:, t * 512:(t + 1) * 512], gw3[:, d * 2 + g, :],
                                      XCB[:, t * 512:(t + 1) * 512])], [gs.R(), XCB.R()], [pg.R()])
                    bg = lambda g: P(f"rbg{e}", d, g, n)
                    kb.act(A_d[:], psr[:, 0:T], AF.Sigmoid, [psr.R(), PAR.R()], [A_d.R()], bias=bg(0))
                    kb.act(BX_d[:], psi[:, 0:T], AF.Sigmoid, [psi.R(), PAR.R()], [BX_d.R()], bias=bg(1))
                    HD = HF if d == 0 else HB
                    kb.act(HD[:], A_d[:], AF.Exp, [A_d.R(), CD.R()], [HD.R()], scale=CD[:, 1, d, n:n + 1])
                    kb.act(A_d[:], A_d[:], AF.Exp, [A_d.R(), CD.R()], [A_d.R()], scale=CD[:, 0, d, n:n + 1])
                    kb.act(HD[:], HD[:], AF.Sqrt, [HD.R(), EPSV.R()], [HD.R()], scale=-1.0, bias=EPSV[:, 1:2])
                    kb.tt(BX_d[:], BX_d[:], HD[:], ALU.mult, [BX_d.R(), HD.R()], [BX_d.R()])
                    kb.tt(BX_d[:], BX_d[:], XC[:], ALU.mult, [BX_d.R(), XC.R()], [BX_d.R()])
                    for si, (s0, Ls) in enumerate(seqs):
                        init = 0.0 if pname == "p" else H0S[:, d, n:n + 1]
                        if d == 0:
                            o_, a_, b_ = HD[:, s0:s0 + Ls], A_d[:, s0:s0 + Ls], BX_d[:, s0:s0 + Ls]
                        else:
                            o_, a_, b_ = (HD[:, s0:s0 + Ls][:, ::-1], A_d[:, s0:s0 + Ls][:, ::-1],
                                          BX_d[:, s0:s0 + Ls][:, ::-1])
                        kb.op("dve", lambda e_: e_.tensor_tensor_scan(out=o_, data0=a_, data1=b_, initial=init,
                                                                      op0=ALU.mult, op1=ALU.add),
                              [A_d.R(), BX_d.R(), H0S.R()], [HD.R()])
                        if pname == "p":
                            col = s0 + Ls - 1 if d == 0 else s0
                            kb.cp(NST[:, e, si, d, n:n + 1], HD[:, col:col + 1], [HD.R()], [NST.R()])
                gring.release(gs)
                if do_ada:
                    ada_run(l + 1, range(n * 12 + 3, n * 12 + 6))
                if pname == "s" and l == 0:
                    ada_run(0, range(48 + n * 6, 48 + n * 6 + 6))
                s = ring.get(("win", pname, l, n, 1))
                ps = psget()
                proj_cols(s, Hh, T, ps)
                ring.release(s)
                kb.act(A_[:], ps[:, 0:T], AF.Gelu_apprx_tanh, [ps.R()], [A_.R()])
                kb.tt(HF[:], HF[:], HB[:], ALU.add, [HF.R(), HB.R()], [HF.R()])
                kb.tt(Y[:, n, :], HF[:], A_[:], ALU.mult, [HF.R(), A_.R()], [Y.R(n)])
                sa = ring.get(("win", pname, l, n, 2))
                psa = psget()
                proj_cols(sa, Hh, T, psa)
                ring.release(sa)
                sb_ = ring.get(("win", pname, l, n, 3))
                psb = psget()
                proj_cols(sb_, Hh, T, psb)
                ring.release(sb_)
                kb.act(HB[:], psb[:, 0:T], AF.Sigmoid, [psb.R()], [HB.R()])
                kb.tt(XCB[:], psa[:, 0:T], HB[:], ALU.mult, [psa.R(), HB.R()], [XCB.R()])
                if do_ada:
                    ada_run(l + 1, range(n * 12 + 6, n * 12 + 9))
                ccw = lambda k: P(f"ccw{e}", n, k)
                pc = psget()
                order = [15] + [k for k in range(31) if k != 15]
                for ti, k in enumerate(order):
                    d = k - 15
                    dg = DG[ti % 4]
                    kb.ts(dg[:], IDF[:], ccw(k), None, ALU.mult, None, [IDF.R(), PAR.R()], [dg.R()])
                    fns = []
                    if ti == 0:
                        for hf in range(T // 512):
                            fns.append(MM(pc[:, hf * 512:(hf + 1) * 512], dg[:], XCB[:, hf * 512:(hf + 1) * 512],
                                          True, False))
                    for (s0_, Ls) in (seqs if ti > 0 else []):
                        lo = s0_ + max(0, -d)
                        hi = s0_ + Ls - max(0, d)
                        for hf in range(T // 512):
                            a = max(lo, hf * 512)
                            b_ = min(hi, (hf + 1) * 512)
                            if b_ > a:
                                fns.append(MM(pc[:, a:b_], dg[:], XCB[:, a + d:b_ + d], ti == 0, ti == 30))
                    kb.mm(fns, [dg.R(), XCB.R()], [pc.R()])
                kb.act(Y[:, 8 + n, :], pc[:, 0:T], AF.Identity, [pc.R(), PAR.R()], [Y.R(8 + n)], bias=P(f"ccb{e}", n))
                if do_ada:
                    ada_run(l + 1, range(n * 12 + 9, n * 12 + 12))
            ps1 = psget()
            ps2 = psget()
            for n in range(8):
                for t in range(T // 512):
                    cs = slice(t * 512, (t + 1) * 512)
                    kb.mm([MM(ps1[:, cs], ONES[:], Y[:, 8 + n, cs], n == 0, n == 7)], [ONES.R(), Y.R(8 + n)], [ps1.R()])
                kb.act(XCB[:], Y[:, 8 + n, :], AF.Square, [Y.R(8 + n)], [XCB.R()])
                for t in range(T // 512):
                    cs = slice(t * 512, (t + 1) * 512)
                    kb.mm([MM(ps2[:, cs], ONES[:], XCB[:, cs], n == 0, n == 7)], [ONES.R(), XCB.R()], [ps2.R()])
            kb.act(A_[:], ps1[:, 0:T], AF.Copy, [ps1.R()], [A_.R()], scale=1.0 / 1024)
            kb.tt(HB[:], A_[:], A_[:], ALU.mult, [A_.R()], [HB.R()])
            kb.stt(BX[:], ps2[:, 0:T], 1.0 / 1024, HB[:], ALU.mult, ALU.subtract, [ps2.R(), HB.R()], [BX.R()])
            kb.act(BX[:], BX[:], AF.Sqrt, [BX.R(), EPSV.R()], [BX.R()], bias=EPSV[:, 0:1])
            kb.op("dve", lambda e_: e_.reciprocal(out=BX[:], in_=BX[:]), [BX.R()], [BX.R()])
            for n in range(8):
                kb.tt(HF[:], Y[:, 8 + n, :], A_[:], ALU.subtract, [Y.R(8 + n), A_.R()], [HF.R()])
                kb.tt(HF[:], HF[:], BX[:], ALU.mult, [HF.R(), BX.R()], [HF.R()])
                kb.act(Y[:, 8 + n, :], HF[:], AF.Silu, [HF.R(), PAR.R()], [Y.R(8 + n)],
                       scale=P(f"lng{e}", n), bias=P(f"lnb{e}", n))

        def odd_mixer(pname, l, T, seqs, Hh, Y, tes):
            o = l // 2
            NTC = T // 128
            QB = kb.sb("QB", [128, T], BF16, tes)
            KBF = kb.sb("KBF", [128, T], BF16, tes)
            VB = kb.sb("VB", [128, NTC, 128], BF16, tes)
            PBs = [kb.sb("PB", [128, NKS], BF16, tes) for _ in range(2)]
            PTs = [kb.sb("PT", [128, 7, 128], BF16, tes) for _ in range(2)]
            MXs = [kb.sb("MX", [128, 4], F32, tes) for _ in range(2)]
            RDs = [kb.sb("RD", [128, 128], F32, tes) for _ in range(2)]
            if pname == "p":
                KF = kb.sb("KF", [128, 512], F32, tes)
                VF = kb.sb("VF", [128, 4, 128], F32, tes)
            elif "units" not in SKIP:
                tring = Ring(kb, 3, [128, NKW], BF16, "tring", tes)
                tring.plan = [p_ for p_ in tplan if p_[0][1] == l]
                kring = Ring(kb, 2, [128, 256], BF16, "kring", tes)
                kring.plan = [p_ for p_ in kplan if p_[0][1] == l]
                vring = Ring(kb, 2, [128, 256], BF16, "vring", tes)
                vring.plan = [p_ for p_ in vplan if p_[0][1] == l]
            for hd in range(NH):
                s = ring.get(("wqkv", pname, l, hd, 0))
                ps = psget(2)
                proj_cols(s, Hh, T, ps)
                ring.release(s)
                kb.act(QB[:], ps[:, 0:T], AF.Copy, [ps.R()], [QB.R()], scale=float(128 ** -0.5))
                do_ada = pname == "s" and l + 1 < nl
                if do_ada:
                    ada_run(l + 1, range(hd * 6, hd * 6 + 2), 2)
                s = ring.get(("wqkv", pname, l, hd, 1))
                ps = psget(2)
                proj_cols(s, Hh, T, ps)
                ring.release(s)
                kb.act(KBF[:], ps[:, 0:T], AF.Copy, [ps.R()], [KBF.R()])
                if pname == "p" and "kvout" not in SKIP:
                    kb.act(KF[:], ps[:, 0:512], AF.Copy, [ps.R()], [KF.R()])
                    kb.dma("sp", nk_d[o, :, hd, :], KF[:], ds_out, reads=[KF.R()])
                if do_ada:
                    ada_run(l + 1, range(hd * 6 + 2, hd * 6 + 4), 2)
                s = ring.get(("wqkv", pname, l, hd, 2))
                s3 = slot3(s)
                ps = psget(2)
                for tc in range(NTC if "vproj" not in SKIP else 0):
                    kb.mm([MM(ps[:, tc * 128:(tc + 1) * 128], Hh[:, k, tc * 128:(tc + 1) * 128], s3[:, k, :],
                              k == 0, k == 15) for k in range(16)], [s.R()] + [Hh.R(k) for k in range(16)], [ps.R()])
                ring.release(s)
                kb.act(VB[:].rearrange("p c d -> p (c d)"), ps[:, 0:T], AF.Copy, [ps.R()], [VB.R()])
                if pname == "p" and "kvout" not in SKIP:
                    kb.act(VF[:].rearrange("p c d -> p (c d)"), ps[:, 0:512], AF.Copy, [ps.R()], [VF.R()])
                    kb.dma("sp", nv_d[o, :, :, hd * 128:(hd + 1) * 128], VF[:], ds_out, reads=[VF.R()])
                if do_ada:
                    ada_run(l + 1, range(hd * 6 + 4, hd * 6 + 6), 2)
                if "units" in SKIP:
                    units = []
                elif pname == "p":
                    units = [(s0 + qq * 128, s0, None) for (s0, Ls) in seqs for qq in range(Ls // 128)]
                else:
                    units = [(qt * 128, QT_R0[qt] * 64, qt) for qt in range(8)]
                    kc_ = kring.get(("kc", l, hd))
                    vc_ = vring.get(("vc", l, hd))
                    vc3 = vc_.t[:].rearrange("p (c d) -> p c d", c=2)

                def stageA(i):
                    q0, k0, qt = units[i]
                    ps = psget(2)
                    qs = QB[:, q0:q0 + 128]
                    if qt is None:
                        kb.mm([MM(ps[:, 0:256], qs, KBF[:, k0:k0 + 256])], [QB.R(), KBF.R()], [ps.R()])
                    else:
                        tb = tring.get(("bt", l, hd, qt))
                        kb.mm([MM(ps[:, 0:512], qs, KBF[:, k0:k0 + 512], True, False),
                               MM(ps[:, 0:512], IDB[:], tb.t[:, 0:512], False, True),
                               MM(ps[:, 512:640], qs, KBF[:, k0 + 512:k0 + 640], True, False),
                               MM(ps[:, 512:640], IDB[:], tb.t[:, 512:640], False, True),
                               MM(ps[:, 640:896], qs, kc_.t[:, :])],
                              [QB.R(), KBF.R(), IDB.R(), tb.R(), kc_.R()], [ps.R()])
                        tring.release(tb)
                    return ps

                def stageB(i, ps):
                    q0, k0, qt = units[i]
                    nks = 256 if qt is None else NKS
                    PB, MX = PBs[i % 2], MXs[i % 2]
                    kb.op("dve", lambda e_: e_.tensor_reduce(out=MX[:, 0:1], in_=ps[:, 0:nks], axis=AX.X, op=ALU.max),
                          [ps.R()], [MX.R()])
                    kb.ts(MX[:, 1:2], MX[:, 0:1], -1.0, None, ALU.mult, None, [MX.R()], [MX.R()])
                    kb.act(PB[:, 0:nks], ps[:, 0:nks], AF.Exp, [ps.R(), MX.R()], [PB.R()], bias=MX[:, 1:2])

                def stageC1(i):
                    q0, k0, qt = units[i]
                    nch = (256 if qt is None else NKS) // 128
                    PB, PT = PBs[i % 2], PTs[i % 2]
                    pst = PSH[4 + i % 2]
                    pstb = pst.ap.bitcast(BF16)
                    kb.mm([(lambda c: lambda pe: pe.transpose(out=pstb[:, c * 128:(c + 1) * 128],
                                                               in_=PB[:, c * 128:(c + 1) * 128], identity=IDB[:]))(c)
                           for c in range(nch)], [PB.R(), IDB.R()], [pst.R()])
                    kb.act(PT[:].rearrange("p c q -> p (c q)")[:, 0:nch * 128], pstb[:, 0:nch * 128], AF.Copy,
                           [pst.R()], [PT.R()])

                def stageC2(i):
                    q0, k0, qt = units[i]
                    nch = (256 if qt is None else NKS) // 128
                    PT = PTs[i % 2]
                    RD = RDs[i % 2]
                    pso = PSH[6 + i % 2]
                    fns = []
                    rds = [PT.R(), VB.R(), ONES.R()]
                    for c in range(nch):
                        if qt is None or c < 5:
                            vch = VB[:, k0 // 128 + c, :]
                        else:
                            vch = vc3[:, c - 5, :]
                        fns.append(MM(pso[:, 0:128], vch, PT[:, c, :], c == 0, c == nch - 1))
                    for c in range(nch):
                        fns.append(MM(pso[:, 128:256], ONES[:], PT[:, c, :], c == 0, c == nch - 1))
                    if qt is not None:
                        rds.append(vc_.R())
                    kb.mm(fns, rds, [pso.R()])
                    kb.op("dve", lambda e_: e_.reciprocal(out=RD[:], in_=pso[:, 128:256]), [pso.R()], [RD.R()])
                    kb.tt(Y[:, hd, q0:q0 + 128], pso[:, 0:128], RD[:], ALU.mult, [pso.R(), RD.R()], [Y.R(hd)])

                nu = len(units)
                for i in range(-1, nu):
                    if i + 1 < nu:
                        ps_ = stageA(i + 1)
                        stageB(i + 1, ps_)
                    if i >= 0:
                        stageC1(i)
                    if i >= 1:
                        stageC2(i - 1)
                if nu:
                    stageC2(nu - 1)
                if pname == "s" and "units" not in SKIP:
                    kring.release(kc_)
                    vring.release(vc_)

        def ffn(pname, l, T, seqs, X, cnd):
            with ExitStack() as fs:
                HFN = kb.sb("HFN", [128, KC, T], BF16, fs)
                ACC = kb.sb("ACC", [128, KC, 512], F32, fs)
                ACTB = [kb.sb(f"ACTB{i}", [128, GRP, 512], BF16, fs) for i in range(2)]
                CGV = [kb.sb("CG", [128, 512], F32, fs), kb.sb("CV", [128, 512], F32, fs)]
                nt = NormTmp(fs)
                derive_ffn(l, cnd)
                premod(X, HFN, T, DER[:, 2, :], MODS[:, l, cnd, 48:64], [MODS.R((l, cnd))], nt, pshget)
                cc = [0]
                starts = [s0 for (s0, Ls) in seqs]
                NW = T // 512
                st = {}

                def wave_geom(w):
                    c0 = w * 512
                    segs = []
                    for (s0, Ls) in seqs:
                        a = max(s0, c0)
                        b_ = min(s0 + Ls, c0 + 512)
                        if b_ > a:
                            segs.append((a - c0, b_ - c0))
                    lh = c0 - 1 if (c0 > 0 and c0 not in starts) else None
                    rh = c0 + 512 if (c0 + 512 < T and (c0 + 512) not in starts) else None
                    return c0, segs, lh, rh

                def chunk_front(w, j):
                    c0, segs, lh, rh = wave_geom(w)
                    pgv = (PSH[2 * (cc[0] % 2)], PSH[2 * (cc[0] % 2) + 1])
                    HT = PSH[6 + cc[0] % 2]
                    cc[0] += 1
                    hcol = (hc[0] % 128) * 4
                    hc[0] += 1
                    halo = lh is not None or rh is not None
                    for part in range(2):
                        s_ = ring.get(("wup", pname, l, w, j, part))
                        s3 = slot3(s_)
                        pp = pgv[part]
                        fns = []
                        for k in range(16):
                            fns.append(MM(pp[:, 0:512], s3[:, k, :], HFN[:, k, c0:c0 + 512], k == 0, k == 15))
                            if lh is not None:
                                fns.append(MM(HT[:, hcol + part * 2:hcol + part * 2 + 1], s3[:, k, :],
                                              HFN[:, k, lh:lh + 1], k == 0, k == 15))
                            if rh is not None:
                                fns.append(MM(HT[:, hcol + part * 2 + 1:hcol + part * 2 + 2], s3[:, k, :],
                                              HFN[:, k, rh:rh + 1], k == 0, k == 15))
                        kb.mm(fns, [s_.R()] + [HFN.R(k) for k in range(16)], [pp.R()] + ([HT.R()] if halo else []))
                        ring.release(s_)
                    st[(w, j)] = (pgv, HT, hcol)

                def chunk_back(w, j, AB, jj):
                    c0, segs, lh, rh = wave_geom(w)
                    pgv, HT, hcol = st.pop((w, j))
                    for part in range(2):
                        pp = pgv[part]
                        CB = CGV[part]
                        c = part * NFF + j
                        u = pp[:, 0:512]
                        w_ = lambda k: P(f"fcw{l}", c, k)
                        kb.act(CB[:], u, AF.Identity, [pp.R(), PAR.R()], [CB.R()], scale=w_(1), bias=P(f"fcb{l}", c))
                        for (a, b_) in segs:
                            if b_ - a > 1:
                                kb.stt(CB[:, a + 1:b_], u[:, a:b_ - 1], w_(0), CB[:, a + 1:b_], ALU.mult, ALU.add,
                                       [pp.R(), CB.R(), PAR.R()], [CB.R()])
                                kb.stt(CB[:, a:b_ - 1], u[:, a + 1:b_], w_(2), CB[:, a:b_ - 1], ALU.mult, ALU.add,
                                       [pp.R(), CB.R(), PAR.R()], [CB.R()])
                        if lh is not None:
                            kb.stt(CB[:, 0:1], HT[:, hcol + part * 2:hcol + part * 2 + 1], w_(0), CB[:, 0:1],
                                   ALU.mult, ALU.add, [HT.R(), CB.R(), PAR.R()], [CB.R()])
                        if rh is not None:
                            kb.stt(CB[:, 511:512], HT[:, hcol + part * 2 + 1:hcol + part * 2 + 2], w_(2),
                                   CB[:, 511:512], ALU.mult, ALU.add, [HT.R(), CB.R(), PAR.R()], [CB.R()])
                    kb.act(CGV[0][:], CGV[0][:], AF.Gelu_apprx_tanh, [CGV[0].R()], [CGV[0].R()])
                    kb.tt(AB[:, jj, :], CGV[0][:], CGV[1][:], ALU.mult, [CGV[0].R(), CGV[1].R()], [AB.R(jj)])

                for w in range(NW):
                    for g in range(NFF // GRP):
                        AB = ACTB[g % 2]
                        for jj in range(GRP):
                            j = g * GRP + jj
                            if (w, j) not in st:
                                chunk_front(w, j)
                            chunk_back(w, j, AB, jj)
                        sds = [ring.get(("wdn", pname, l, w, g * GRP + jj)) for jj in range(GRP)]
                        for m in range(16):
                            ps = pshget()
                            kb.mm([MM(ps[:, 0:512], sds[jj].t[:, m * 128:(m + 1) * 128], AB[:, jj, :], jj == 0,
                                      jj == GRP - 1) for jj in range(GRP)],
                                  [sd.R() for sd in sds] + [AB.R(jj) for jj in range(GRP)], [ps.R()])
                            if g == 0:
                                kb.cp(ACC[:, m, :], ps[:, 0:512], [ps.R()], [ACC.R(m)])
                            else:
                                kb.tt(ACC[:, m, :], ACC[:, m, :], ps[:, 0:512], ALU.add, [ACC.R(m), ps.R()], [ACC.R(m)])
                        for sd in sds:
                            ring.release(sd)
                    if w + 1 < NW:
                        chunk_front(w + 1, 0)
                        chunk_front(w + 1, 1)
                    post_residual(ACC, X, slice(w * 512, (w + 1) * 512), DER[:, 3, :], nt, pshget)
                kb.barrier()

        for pname, T, seqs in passes:
            cnd = 0 if pname == "p" else 1
            with ExitStack() as pes:
                X = kb.sb("X" + pname, [128, KC, T], F32, pes)
                xsrc = xp_d if pname == "p" else xs_d
                for k in range(KC):
                    kb.dma("sp", X[:, k, :], xsrc[:, k, :], ds_x, writes=[X.R(k)])
                for k in range(KC):
                    X.R(k).w = ("d", ds_x, ds_x["c"])
                for l in range(nl):
                    derive(l, cnd)
                    with ExitStack() as ys:
                        Y = kb.sb("Y", [128, KC, T], BF16, ys)
                        with ExitStack() as hs:
                            Hh = kb.sb("H", [128, KC, T], BF16, hs)
                            with ExitStack() as ns:
                                nt = NormTmp(ns)
                                premod(X, Hh, T, DER[:, 0, :], MODS[:, l, cnd, 0:16], [MODS.R((l, cnd))], nt)
                                kb.barrier()
                            with ExitStack() as tes:
                                if l % 2 == 0:
                                    even_mixer(pname, l, T, seqs, Hh, Y, tes)
                                else:
                                    odd_mixer(pname, l, T, seqs, Hh, Y, tes)
                                kb.barrier()
                        with ExitStack() as os_:
                            OB = kb.sb("OB", [128, KC, 512], F32, os_)
                            nt = NormTmp(os_)
                            outproj_residual(pname, l, T, X, Y, OB, nt)
                            kb.barrier()
                    ffn(pname, l, T, seqs, X, cnd)
                ydst = yp_d if pname == "p" else ys_d
                for k in range(KC):
                    kb.dma("sp", ydst[:, k, :], X[:, k, :], ds_out, reads=[X.R(k)])
                if pname == "p":
                    for e in range(2):
                        kb.dma("sp", nst_d[e], NST[:, e].rearrange("p s d n -> p (s d n)"), ds_out, reads=[NST.R()])
                kb.barrier()
        kb.barrier()
    return nc


_PROG = {}


def _btab(rel):
    tab = np.full((2, NH, 5, 128, NKW), -1e30, np.float32)
    for vi, qt in enumerate([0, 1, 2, 6, 7]):
        r0, R0 = 2 * qt, QT_R0[qt]
        q = np.arange(128)
        r = r0 + q // 64
        c = q % 64
        kk = np.arange(NKW)
        kr = R0 + kk // 64
        kc = kk % 64
        rsr = np.clip(r - 4, 0, 8)
        csc = np.clip(c - 8, 0, 48)
        valid = ((kr[None, :] >= rsr[:, None]) & (kr[None, :] < rsr[:, None] + 8) &
                 (kc[None, :] >= csc[:, None]) & (kc[None, :] < csc[:, None] + 16))
        dr = np.clip(kr[None, :] - r[:, None] + 7, 0, 14)
        dc = np.clip(kc[None, :] - c[:, None], -15, 15) + 15
        vals = rel[:, :, dr, dc]
        tab[:, :, vi] = np.where(valid[None, None], vals, np.float32(-1e30))
    return tab


def kernel(**inp):
    nl = int(os.environ.get("MK_NL", L))
    g = {k: np.asarray(v) for k, v in inp.items()}
    poff, NP = param_layout()
    par = np.zeros((128, NP), np.float32)

    def put(name, arr):
        o, shape = poff[name]
        par[:, o:o + int(np.prod(shape))] = np.ascontiguousarray(arr).reshape(128, -1)
    for l in range(L):
        put(f"b_ada{l}", g["b_ada"][l].reshape(96, 128).T)
        put(f"norm_g{l}", g["norm_g"][l].reshape(4, 16, 128).transpose(2, 0, 1))
        put(f"fcw{l}", g["ffn_conv_w"][l].reshape(3, 88, 128).transpose(2, 1, 0))
        put(f"fcb{l}", g["ffn_conv_b"][l].reshape(88, 128).T)
    for e in range(2):
        put(f"rcw{e}", g["rg_conv_w"][e].reshape(4, 8, 128).transpose(2, 1, 0))
        put(f"rcb{e}", g["rg_conv_b"][e].reshape(8, 128).T)
        put(f"rbg{e}", g["rg_b_gate"][e].reshape(2, 2, 8, 128).transpose(3, 0, 1, 2))
        put(f"lam{e}", g["rg_lambda"][e].reshape(2, 8, 128).transpose(2, 0, 1))
        put(f"ccw{e}", g["cf_conv_w"][e].reshape(31, 8, 128).transpose(2, 1, 0))
        put(f"ccb{e}", g["cf_conv_b"][e].reshape(8, 128).T)
        put(f"lng{e}", g["cf_ln_g"][e].reshape(8, 128).T)
        put(f"lnb{e}", g["cf_ln_b"][e].reshape(8, 128).T)
    btab = _btab(g["na_rel_bias"])
    ident = np.eye(128, dtype=np.float32)
    shared = {"par": par, "btab": btab, "ident": ident}
    def tile_cols(W):
        lead = W.shape[:-2]
        N = W.shape[-1]
        nl_ = len(lead)
        Wt = W.reshape(*lead, 16, 128, N // 128, 128)
        perm = tuple(range(nl_)) + (nl_ + 2, nl_ + 1, nl_ + 0, nl_ + 3)
        return np.ascontiguousarray(Wt.transpose(perm)).reshape(*lead, N // 128, 128, 16 * 128)
    for k in ("w_ada", "w_in_even", "w_out_even", "w_qkv_odd", "w_out_odd", "w_ffn_up"):
        shared[k] = tile_cols(np.asarray(g[k], dtype=np.float32))
    for k in ("rg_w_gate", "w_ffn_down"):
        shared[k] = np.ascontiguousarray(g[k], dtype=np.float32)

    def fm(x):
        return np.ascontiguousarray(x.reshape(x.shape[0], KC, 128).transpose(2, 1, 0))
    in_maps = []
    for i in range(NCORES):
        m = dict(shared)
        m["xp"] = fm(g["x_prompt"][2 * i:2 * i + 2].reshape(512, D))
        m["xs"] = fm(g["x_sample"][i])
        cond = np.stack([g["c_ctx"], g["c"][i]], axis=-1)
        m["cond"] = np.ascontiguousarray(cond.reshape(KC, 128, 2).transpose(1, 0, 2))
        m["kctx"] = np.ascontiguousarray(g["cache_k"][i].transpose(0, 3, 2, 1))
        m["vctx"] = np.ascontiguousarray(g["cache_v"][i].reshape(2, 2, 128, D).transpose(0, 2, 1, 3))
        m["h0"] = np.ascontiguousarray(g["state_rglru"][i].reshape(2, 2, 8, 128).transpose(0, 3, 1, 2))
        in_maps.append(m)
    key = nl
    if key not in _PROG:
        mp = os.environ.get("MK_PASS", "ps")
        _PROG[key] = build_program(nl, do_sample="s" in mp, do_prompt="p" in mp)
    ncr = int(os.environ.get("MK_CORES", NCORES))
    res = run_bass_kernel_spmd(_PROG[key], in_maps[:ncr], core_ids=list(range(ncr)))
    yp = np.zeros((16, 256, D), np.float32)
    ys = np.zeros((8, 1024, D), np.float32)
    nk = np.zeros((16, 2, 256, NH, 128), np.float32)
    nv = np.zeros((16, 2, 256, NH, 128), np.float32)
    nst = np.zeros((16, 2, 2, 1024), np.float32)
    for i, r in enumerate(res.results):
        yp[2 * i:2 * i + 2] = r["yp"].transpose(2, 1, 0).reshape(2, 256, D)
        ys[i] = r["ys"].transpose(2, 1, 0).reshape(1024, D)
        k_ = r["nk"].transpose(0, 3, 2, 1).reshape(2, 2, 256, NH, 128)
        nk[2 * i:2 * i + 2] = k_.transpose(1, 0, 2, 3, 4)
        v_ = r["nv"].transpose(0, 2, 1, 3).reshape(2, 2, 256, NH, 128)
        nv[2 * i:2 * i + 2] = v_.transpose(1, 0, 2, 3, 4)
        s_ = r["nst"].transpose(2, 0, 3, 4, 1).reshape(2, 2, 2, 1024)
        nst[2 * i:2 * i + 2] = s_
    return yp, ys, nk, nv, nst
```

```python
import os
from contextlib import ExitStack
import numpy as np
import concourse.bass as bass
import concourse.mybir as mybir
from concourse.bass_utils import run_bass_kernel_spmd

F32, BF16 = mybir.dt.float32, mybir.dt.bfloat16
AF = mybir.ActivationFunctionType
ALU = mybir.AluOpType
AX = mybir.AxisListType

NCORES = 8
L = 4
D = 2048
KC = 16
DFF = 5632
NFF = 44
GRP = 4
NH = 16
EPS = 1e-6
NRING = 7
WIN_ROWS = 10
NKW = WIN_ROWS * 64
NKS = NKW + 256
QT_R0 = [0, 0, 0, 2, 4, 6, 6, 6]
QT_VAR = [0, 1, 2, 2, 2, 2, 3, 4]


def param_layout():
    off = {}
    n = 0

    def add(name, *shape):
        nonlocal n
        off[name] = (n, shape)
        n += int(np.prod(shape))
    for l in range(L):
        add(f"b_ada{l}", 96)
        add(f"norm_g{l}", 4, 16)
        add(f"fcw{l}", 88, 3)
        add(f"fcb{l}", 88)
    for e in range(2):
        add(f"rcw{e}", 8, 4)
        add(f"rcb{e}", 8)
        add(f"rbg{e}", 2, 2, 8)
        add(f"lam{e}", 2, 8)
        add(f"ccw{e}", 8, 31)
        add(f"ccb{e}", 8)
        add(f"lng{e}", 8)
        add(f"lnb{e}", 8)
    return off, n


class Res:
    __slots__ = ("w", "r", "excl")

    def __init__(self, excl=False):
        self.w = None
        self.r = []
        self.excl = excl


class TB:
    def __init__(self, t, excl=False):
        self.t = t
        self.res = {}
        self.excl = excl

    def R(self, key=0):
        r = self.res.get(key)
        if r is None:
            r = self.res[key] = Res(self.excl)
        return r

    def __getitem__(self, k):
        return self.t[k]


class KB:
    def __init__(self, nc, es):
        self.nc = nc
        self.es = es
        self.eng = {"pe": nc.tensor, "act": nc.scalar, "dve": nc.vector, "pool": nc.gpsimd, "sp": nc.sync}
        self.sem = {e: es.enter_context(nc.semaphore("s_" + e)) for e in self.eng}
        self.cnt = {e: 0 for e in self.eng}
        self.seen = {e: {} for e in self.eng}
        self.dsems = []
        self.nd = 0

    def sb(self, name, shape, dt, es=None):
        self.nd += 1
        return TB((es or self.es).enter_context(self.nc.sbuf_tensor(f"{name}_{self.nd}", list(shape), dt)))

    def dsem(self, name, ring=False):
        self.nd += 1
        s = self.es.enter_context(self.nc.semaphore(f"{name}_{self.nd}"))
        h = {"s": s, "c": 0, "id": len(self.dsems), "ring": ring}
        self.dsems.append(h)
        return h

    def _wait(self, eng, dep):
        if dep[0] == "e":
            key = ("e", dep[1])
            semh = self.sem[dep[1]]
        else:
            key = ("d", dep[1]["id"])
            semh = dep[1]["s"]
        val = dep[2]
        if self.seen[eng].get(key, 0) >= val:
            return
        self.seen[eng][key] = val
        self.eng[eng].wait_ge(semh, val)

    def _deps(self, eng, reads, writes):
        for r in reads:
            if r.w is not None:
                self._wait_f(eng, r.w)
            if r.excl:
                for d in r.r:
                    if not (d[0] == "e" and d[1] == eng):
                        self._wait_f(eng, d)
        for w in writes:
            if w.w is not None:
                self._wait_f(eng, w.w)
            for d in w.r:
                self._wait_f(eng, d)

    def _wait_f(self, eng, d):
        if eng == "pe" and d[0] == "e" and d[1] == "pe":
            return
        self._wait(eng, d)

    def _mark(self, me, reads, writes):
        for w in writes:
            w.w = me
            w.r = []
        for r in reads:
            if r in writes:
                continue
            r.r.append(me)
            if len(r.r) > 24:
                last = {}
                for d in r.r:
                    k = (d[0], d[1] if d[0] == "e" else d[1]["id"])
                    if k not in last or last[k][2] < d[2]:
                        last[k] = d
                r.r = list(last.values())

    def op(self, eng, fn, reads=(), writes=()):
        self._deps(eng, reads, writes)
        ins = fn(self.eng[eng])
        ins.then_inc(self.sem[eng], 1)
        self.cnt[eng] += 1
        self._mark(("e", eng, self.cnt[eng]), reads, writes)

    def mm(self, fns, reads=(), writes=()):
        self._deps("pe", reads, writes)
        ins = None
        for f in fns:
            ins = f(self.nc.tensor)
        ins.then_inc(self.sem["pe"], 1)
        self.cnt["pe"] += 1
        self._mark(("e", "pe", self.cnt["pe"]), reads, writes)

    def dma(self, q, out, in_, ds, reads=(), writes=()):
        self._deps(q, reads, writes)
        self.eng[q].dma_start(out=out, in_=in_).then_inc(ds["s"], 16)
        ds["c"] += 16
        self._mark(("d", ds, ds["c"]), reads, writes)

    def barrier(self, full=False):
        for e in self.eng:
            if e == "pool" and not full:
                continue
            for e2 in self.eng:
                if e2 != e and self.cnt[e2] > 0 and (full or e2 != "pool"):
                    self._wait(e, ("e", e2, self.cnt[e2]))
            for ds in self.dsems:
                if ds["c"] > 0 and (full or not ds["ring"]):
                    self._wait(e, ("d", ds, ds["c"]))

    def act(self, out, in_, func, reads, writes, **kw):
        self.op("act", lambda e: e.activation(out=out, in_=in_, func=func, **kw), reads, writes)

    def tt(self, out, a, b, op, reads, writes, eng="dve"):
        self.op(eng, lambda e: e.tensor_tensor(out=out, in0=a, in1=b, op=op), reads, writes)

    def stt(self, out, in0, scalar, in1, op0, op1, reads, writes, eng="dve"):
        self.op(eng, lambda e: e.scalar_tensor_tensor(out=out, in0=in0, scalar=scalar, in1=in1, op0=op0, op1=op1),
                reads, writes)

    def ts(self, out, in0, s1, s2, op0, op1, reads, writes, eng="dve"):
        if s2 is None:
            self.op(eng, lambda e: e.tensor_scalar(out=out, in0=in0, scalar1=s1, scalar2=None, op0=op0), reads, writes)
        else:
            self.op(eng, lambda e: e.tensor_scalar(out=out, in0=in0, scalar1=s1, scalar2=s2, op0=op0, op1=op1),
                    reads, writes)

    def cp(self, out, in_, reads, writes, eng="dve"):
        self.op(eng, lambda e: e.tensor_copy(out=out, in_=in_), reads, writes)


class Ring:
    def __init__(self, kb, n, shape, dt, name, es=None):
        self.kb = kb
        self.n = n
        self.slots = [kb.sb(f"{name}{i}", shape, dt, es) for i in range(n)]
        self.ds = [kb.dsem(f"{name}_d{i}", ring=True) for i in range(n)]
        self.held = [False] * n
        self.plan = []
        self.issued = 0
        self.consumed = 0

    def _pump(self):
        while self.issued < len(self.plan):
            i = self.issued
            sl = i % self.n
            if self.held[sl]:
                break
            s = self.slots[sl]
            out, in_ = self.plan[i][1](s)
            self.kb.dma("pool", out, in_, self.ds[sl], writes=[s.R()])
            self.held[sl] = True
            self.issued += 1

    def get(self, tag):
        c = self.consumed
        assert self.plan[c][0] == tag, (c, self.plan[c][0], tag)
        self._pump()
        assert self.issued > c, ("ring starved", tag)
        self.consumed += 1
        return self.slots[c % self.n]

    def release(self, s):
        self.held[self.slots.index(s)] = False
        self._pump()


def build_program(nl=L, do_sample=True, do_prompt=True):
    nc = bass.Bass("TRN2", target_bir_lowering=False)
    poff, NP = param_layout()

    def din(name, shape):
        return nc.dram_tensor(name, list(shape), F32, kind="ExternalInput").ap()

    def dout(name, shape):
        return nc.dram_tensor(name, list(shape), F32, kind="ExternalOutput").ap()

    xp_d = din("xp", [128, KC, 512])
    xs_d = din("xs", [128, KC, 1024])
    cond_d = din("cond", [128, KC, 2])
    kctx_d = din("kctx", [2, 128, NH, 256])
    vctx_d = din("vctx", [2, 128, 2, D])
    h0_d = din("h0", [2, 128, 2, 8])
    par_d = din("par", [128, NP])
    btab_d = din("btab", [2, NH, 5, 128, NKW])
    w_ada = din("w_ada", [L, 96, 128, 16 * 128])
    w_in = din("w_in_even", [2, 32, 128, 16 * 128])
    w_gate = din("rg_w_gate", [2, 2, 2, 8, 128, 128])
    w_oute = din("w_out_even", [2, 16, 128, 16 * 128])
    w_qkv = din("w_qkv_odd", [2, 48, 128, 16 * 128])
    w_outo = din("w_out_odd", [2, 16, 128, 16 * 128])
    w_up = din("w_ffn_up", [L, 88, 128, 16 * 128])
    w_down = din("w_ffn_down", [L, DFF, D])

    yp_d = dout("yp", [128, KC, 512])
    ys_d = dout("ys", [128, KC, 1024])
    nk_d = dout("nk", [2, 128, NH, 512])
    nv_d = dout("nv", [2, 128, 4, D])
    nst_d = dout("nst", [2, 128, 2, 2, 8])

    es = ExitStack()
    with es:
        kb = KB(nc, es)
        PAR = kb.sb("PAR", [128, NP], F32)
        MODS = kb.sb("MODS", [128, L, 2, 96], F32)
        DER = kb.sb("DER", [128, 4, 16], F32)
        CD = kb.sb("CD", [128, 2, 2, 8], F32)
        CONDF = kb.sb("CONDF", [128, KC, 2], F32)
        CONDB = kb.sb("CONDB", [128, KC, 2], BF16)
        ONES = kb.sb("ONES", [128, 128], BF16)
        IDB = kb.sb("IDB", [128, 128], BF16)
        IDF = kb.sb("IDF", [128, 128], F32)
        NST = kb.sb("NST", [128, 2, 2, 2, 8], F32)
        ring = Ring(kb, NRING, [128, 2048], BF16, "ring")
        tplan = []
        PS = [TB(es.enter_context(nc.psum_tensor(f"ps{i}", [128, 1024], F32)), excl=True) for i in range(4)]
        ps_i = [0]

        def psget(n=4):
            t = PS[ps_i[0] % n]
            ps_i[0] += 1
            return t

        ds_misc = kb.dsem("misc")
        ds_gw = kb.dsem("gw")
        ds_x = kb.dsem("x")
        ds_out = kb.dsem("out")
        ds_ctx = kb.dsem("ctx")

        def P(name, *idx):
            o, shape = poff[name]
            strides = [int(np.prod(shape[i + 1:])) for i in range(len(shape))]
            for i, s in zip(idx, strides):
                o += i * s
            rem = int(np.prod(shape[len(idx):])) if len(idx) < len(shape) else 1
            return PAR[:, o:o + rem]

        def colblk(w3d, c0):
            src = w3d[c0 // 128]
            return lambda s: (s.t[:], src)

        def rowblk(w2d, r0):
            src = w2d[r0:r0 + 128, :]
            return lambda s: (s.t[:], src)

        plan = ring.plan
        ada_pre = [0] if do_sample else list(range(nl))
        pre_cbs = range(48) if do_sample else range(96)
        for l in ada_pre:
            for cb in pre_cbs:
                plan.append((("ada", l, cb), colblk(w_ada[l], cb * 128)))
        passes = ([("s", 1024, [(0, 1024)])] if do_sample else []) + \
                 ([("p", 512, [(0, 256), (256, 256)])] if do_prompt else [])

        def plan_ada(l2, cbs):
            for cb in cbs:
                plan.append((("ada", l2, cb), colblk(w_ada[l2], cb * 128)))
        for pname, T, seqs in passes:
            for l in range(nl):
                if l % 2 == 0:
                    e = l // 2
                    for n in range(8):
                        do_ada = pname == "s" and l + 1 < nl
                        plan.append((("win", pname, l, n, 0), colblk(w_in[e], 0 * 1024 + n * 128)))
                        if do_ada:
                            plan_ada(l + 1, range(n * 12, n * 12 + 6))
                        if pname == "s" and l == 0:
                            plan_ada(0, range(48 + n * 6, 48 + n * 6 + 6))
                        for part in range(1, 4):
                            plan.append((("win", pname, l, n, part), colblk(w_in[e], part * 1024 + n * 128)))
                        if do_ada:
                            plan_ada(l + 1, range(n * 12 + 6, n * 12 + 12))
                    for w in range(T // 512):
                        for m in range(16):
                            plan.append((("wout", pname, l, w, m), colblk(w_oute[e], m * 128)))
                else:
                    o = l // 2
                    for hd in range(NH):
                        for part in range(3):
                            plan.append((("wqkv", pname, l, hd, part), colblk(w_qkv[o], part * D + hd * 128)))
                            if pname == "s" and l + 1 < nl:
                                plan_ada(l + 1, range(hd * 6 + part * 2, hd * 6 + part * 2 + 2))
                        if pname == "s":
                            for qt in range(8):
                                src = btab_d[o, hd, QT_VAR[qt]]
                                tplan.append((("bt", l, hd, qt), (lambda src: lambda s: (s.t[:], src))(src)))
                    for w in range(T // 512):
                        for m in range(16):
                            plan.append((("wout", pname, l, w, m), colblk(w_outo[o], m * 128)))
                for w in range(T // 512):
                    for g in range(NFF // GRP):
                        for jj in range(GRP):
                            j = g * GRP + jj
                            plan.append((("wup", pname, l, w, j, 0), colblk(w_up[l], j * 128)))
                            plan.append((("wup", pname, l, w, j, 1), colblk(w_up[l], DFF + j * 128)))
                        for jj in range(GRP):
                            j = g * GRP + jj
                            plan.append((("wdn", pname, l, w, j), rowblk(w_down[l], j * 128)))

        def slot3(s):
            return s.t[:].rearrange("p (k n) -> p k n", k=16)

        SKIP = os.environ.get("MK_SKIP", "").split(",")
        ident_d = din("ident", [128, 128])
        EPSV = kb.sb("EPSV", [128, 2], F32)
        H0S = kb.sb("H0S", [128, 2, 8], F32)
        kplan, vplan = [], []
        if "units" in SKIP:
            tplan.clear()
        if do_sample and "units" not in SKIP:
            for l in range(nl):
                if l % 2 == 1:
                    o = l // 2
                    for hd in range(NH):
                        ksrc = kctx_d[o, :, hd, :]
                        vsrc = vctx_d[o, :, :, hd * 128:(hd + 1) * 128]
                        kplan.append((("kc", l, hd), (lambda src: lambda s: (s.t[:], src))(ksrc)))
                        vplan.append((("vc", l, hd), (lambda src: lambda s: (
                            s.t[:].rearrange("p (c d) -> p c d", c=2), src))(vsrc)))
        hc = [0]

        class PV:
            def __init__(self, tb, half):
                self.ap = tb.t[:, half * 512:(half + 1) * 512]
                self.res = Res(excl=True)

            def R(self):
                return self.res

            def __getitem__(self, k):
                return self.ap[k]
        PSH = [PV(PS[i // 2], i % 2) for i in range(8)]
        psh_i = [0]

        def pshget():
            t = PSH[4 + psh_i[0] % 2]
            psh_i[0] += 1
            return t

        def MM(out, lhsT, rhs, start=True, stop=True):
            return lambda pe: pe.matmul(out, lhsT=lhsT, rhs=rhs, start=start, stop=stop)

        kb.dma("sp", PAR[:], par_d, ds_misc, writes=[PAR.R()])
        kb.dma("sp", CONDF[:], cond_d, ds_misc, writes=[CONDF.R()])
        kb.dma("sp", IDF[:], ident_d, ds_misc, writes=[IDF.R()])
        for t_ in (PAR, CONDF, IDF):
            t_.R().w = ("d", ds_misc, ds_misc["c"])
        kb.op("dve", lambda e: e.memset(ONES[:], 1.0), writes=[ONES.R()])
        kb.op("dve", lambda e: e.memset(EPSV[:, 0:1], EPS), writes=[EPSV.R()])
        kb.op("dve", lambda e: e.memset(EPSV[:, 1:2], 1.0), writes=[EPSV.R()])
        kb.op("dve", lambda e: e.memset(NST[:], 0.0), writes=[NST.R()])
        kb.cp(IDB[:], IDF[:], [IDF.R()], [IDB.R()])
        kb.act(CONDB[:], CONDF[:], AF.Silu, [CONDF.R()], [CONDB.R()])

        def ada_run(l2, cbs, nps=4):
            for cb in cbs:
                s = ring.get(("ada", l2, cb))
                s3 = slot3(s)
                ps = psget(nps)
                kb.mm([MM(ps[:, 0:2], s3[:, k, :], CONDB[:, k, :], k == 0, k == 15) for k in range(16)],
                      [s.R(), CONDB.R()], [ps.R()])
                ring.release(s)
                kb.ts(MODS[:, l2, :, cb], ps[:, 0:2], P(f"b_ada{l2}")[:, cb:cb + 1], None, ALU.add, None,
                      [ps.R(), PAR.R()], [MODS.R((l2, 0)), MODS.R((l2, 1))])

        for l in ada_pre:
            ada_run(l, pre_cbs)

        def derive(l, cnd):
            md = lambda m: MODS[:, l, cnd, m * 16:(m + 1) * 16]
            rd = [MODS.R((l, cnd)), PAR.R()]
            kb.stt(DER[:, 0, :], md(1), 1.0, P(f"norm_g{l}", 0), ALU.add, ALU.mult, rd, [DER.R()])
            kb.tt(DER[:, 1, :], md(2), P(f"norm_g{l}", 1), ALU.mult, rd, [DER.R()])

        def derive_ffn(l, cnd):
            md = lambda m: MODS[:, l, cnd, m * 16:(m + 1) * 16]
            rd = [MODS.R((l, cnd)), PAR.R()]
            kb.stt(DER[:, 2, :], md(4), 1.0, P(f"norm_g{l}", 2), ALU.add, ALU.mult, rd, [DER.R()])
            kb.tt(DER[:, 3, :], md(5), P(f"norm_g{l}", 3), ALU.mult, rd, [DER.R()])

        class NormTmp:
            def __init__(self, es_):
                self.RS = kb.sb("RS", [128, 512], F32, es_)
                self.SQ = [kb.sb(f"SQ{i}", [128, 512], BF16, es_) for i in range(4)]
                self.TM = [kb.sb(f"TM{i}", [128, 512], F32, es_) for i in range(4)]

        def rstd_of(getsrc, srcres, nt, getps=None):
            ps = (getps or psget)()
            for k in range(KC):
                sq = nt.SQ[k % 4]
                if k % 2 == 0:
                    kb.act(sq[:], getsrc(k), AF.Square, [srcres(k)], [sq.R()])
                else:
                    kb.tt(sq[:], getsrc(k), getsrc(k), ALU.mult, [srcres(k)], [sq.R()])
                kb.mm([MM(ps[:, 0:512], ONES[:], sq[:], k == 0, k == KC - 1)], [ONES.R(), sq.R()], [ps.R()])
            kb.act(nt.RS[:], ps[:, 0:512], AF.Ln, [ps.R(), EPSV.R()], [nt.RS.R()], scale=1.0 / D, bias=EPSV[:, 0:1])
            kb.act(nt.RS[:], nt.RS[:], AF.Exp, [nt.RS.R()], [nt.RS.R()], scale=-0.5)

        def premod(X, Hh, T, Avec, Bvec, extra_reads, nt, getps=None):
            for t in range(T // 512):
                cs = slice(t * 512, (t + 1) * 512)
                rstd_of(lambda k: X[:, k, cs], lambda k: X.R(k), nt, getps)
                for k in range(KC):
                    tm = nt.TM[k % 4]
                    kb.tt(tm[:], X[:, k, cs], nt.RS[:], ALU.mult, [X.R(k), nt.RS.R()], [tm.R()])
                    kb.act(Hh[:, k, cs], tm[:], AF.Identity, [tm.R(), DER.R()] + extra_reads, [Hh.R(k)],
                           scale=Avec[:, k:k + 1], bias=Bvec[:, k:k + 1])

        def post_residual(OB, X, cs, Gvec, nt, getps=None):
            rstd_of(lambda k: OB[:, k, :], lambda k: OB.R(k), nt, getps)
            for k in range(KC):
                tm = nt.TM[k % 4]
                kb.tt(tm[:], OB[:, k, :], nt.RS[:], ALU.mult, [OB.R(k), nt.RS.R()], [tm.R()])
                kb.stt(X[:, k, cs], tm[:], Gvec[:, k:k + 1], X[:, k, cs], ALU.mult, ALU.add,
                       [tm.R(), X.R(k), DER.R()], [X.R(k)])

        def proj_cols(s, Hh, T, ps):
            s3 = slot3(s)
            for t in range(T // 512):
                kb.mm([MM(ps[:, t * 512:(t + 1) * 512], s3[:, k, :], Hh[:, k, t * 512:(t + 1) * 512], k == 0, k == 15)
                       for k in range(16)], [s.R()] + [Hh.R(k) for k in range(16)], [ps.R()])

        def outproj_residual(pname, l, T, X, Y, OB, nt):
            for w in range(T // 512):
                cs = slice(w * 512, (w + 1) * 512)
                for m in range(16):
                    s = ring.get(("wout", pname, l, w, m))
                    s3 = slot3(s)
                    ps = psget()
                    kb.mm([MM(ps[:, 0:512], s3[:, k, :], Y[:, k, cs], k == 0, k == 15) for k in range(16)],
                          [s.R()] + [Y.R(k) for k in range(16)], [ps.R()])
                    ring.release(s)
                    kb.act(OB[:, m, :], ps[:, 0:512], AF.Copy, [ps.R()], [OB.R(m)])
                post_residual(OB, X, cs, DER[:, 1, :], nt)

        def even_mixer(pname, l, T, seqs, Hh, Y, tes):
            e = l // 2
            XC = kb.sb("XC", [128, T], F32, tes)
            XCB = kb.sb("XCB", [128, T], BF16, tes)
            HF = kb.sb("HF", [128, T], F32, tes)
            HB = kb.sb("HB", [128, T], F32, tes)
            A_ = kb.sb("A_", [128, T], F32, tes)
            BX = kb.sb("BX", [128, T], F32, tes)
            A2 = kb.sb("A2", [128, T], F32, tes)
            BX2 = kb.sb("BX2", [128, T], F32, tes)
            DG = [kb.sb("DG", [128, 128], BF16, tes) for _ in range(4)]
            gring = Ring(kb, 2, [128, 512], BF16, "gring", tes)
            for n_ in range(8):
                gsrc = w_gate[e][:, :, n_].rearrange("d g k j -> k (d g) j")
                gring.plan.append((("gw", n_), (lambda src: lambda s: (
                    s.t[:].rearrange("k (b j) -> k b j", b=4), src))(gsrc)))
            if pname == "s":
                kb.dma("sp", H0S[:], h0_d[e], ds_ctx, writes=[H0S.R()])
            lam = P(f"lam{e}")
            CDf = CD[:].rearrange("p a d n -> p a (d n)")
            kb.act(CDf[:, 0, :], lam, AF.Exp, [PAR.R()], [CD.R()], scale=-1.0)
            kb.act(CDf[:, 0, :], CDf[:, 0, :], AF.Ln, [CD.R(), EPSV.R()], [CD.R()], bias=EPSV[:, 1:2])
            kb.ts(CDf[:, 1, :], CDf[:, 0, :], -16.0, None, ALU.mult, None, [CD.R()], [CD.R()])
            kb.ts(CDf[:, 0, :], CDf[:, 0, :], -8.0, None, ALU.mult, None, [CD.R()], [CD.R()])
            for n in range(8):
                s = ring.get(("win", pname, l, n, 0))
                ps = psget()
                proj_cols(s, Hh, T, ps)
                ring.release(s)
                do_ada = pname == "s" and l + 1 < nl
                if do_ada:
                    ada_run(l + 1, range(n * 12, n * 12 + 3))
                rcw = lambda k: P(f"rcw{e}", n, k)
                kb.act(XC[:], ps[:, 0:T], AF.Identity, [ps.R(), PAR.R()], [XC.R()], scale=rcw(2), bias=P(f"rcb{e}", n))
                for k in (0, 1, 3):
                    d = k - 2
                    for (s0, Ls) in seqs:
                        lo = s0 + max(0, -d)
                        hi = s0 + Ls - max(0, d)
                        kb.stt(XC[:, lo:hi], ps[:, lo + d:hi + d], rcw(k), XC[:, lo:hi], ALU.mult, ALU.add,
                               [ps.R(), XC.R(), PAR.R()], [XC.R()])
                kb.act(XCB[:], XC[:], AF.Copy, [XC.R()], [XCB.R()])
                gs = gring.get(("gw", n))
                gw3 = gs.t[:].rearrange("k (b j) -> k b j", b=4)
                for d in range(2):
                    A_d, BX_d = (A_, BX) if d == 0 else (A2, BX2)
                    psr = psget()
                    psi = psget()
                    for g, pg in ((0, psr), (1, psi)):
                        for t in range(T // 512):
                            kb.mm([MM(pg[:, t * 512:(t + 1) * 512], gw3[:, d * 2 + g, :],
                                      XCB[:, t * 512:(t + 1) * 512])], [gs.R(), XCB.R()], [pg.R()])
                    bg = lambda g: P(f"rbg{e}", d, g, n)
                    kb.act(A_d[:], psr[:, 0:T], AF.Sigmoid, [psr.R(), PAR.R()], [A_d.R()], bias=bg(0))
                    kb.act(BX_d[:], psi[:, 0:T], AF.Sigmoid, [psi.R(), PAR.R()], [BX_d.R()], bias=bg(1))
                    HD = HF if d == 0 else HB
                    kb.act(HD[:], A_d[:], AF.Exp, [A_d.R(), CD.R()], [HD.R()], scale=CD[:, 1, d, n:n + 1])
                    kb.act(A_d[:], A_d[:], AF.Exp, [A_d.R(), CD.R()], [A_d.R()], scale=CD[:, 0, d, n:n + 1])
                    kb.act(HD[:], HD[:], AF.Sqrt, [HD.R(), EPSV.R()], [HD.R()], scale=-1.0, bias=EPSV[:, 1:2])
                    kb.tt(BX_d[:], BX_d[:], HD[:], ALU.mult, [BX_d.R(), HD.R()], [BX_d.R()])
                    kb.tt(BX_d[:], BX_d[:], XC[:], ALU.mult, [BX_d.R(), XC.R()], [BX_d.R()])
                    for si, (s0, Ls) in enumerate(seqs):
                        init = 0.0 if pname == "p" else H0S[:, d, n:n + 1]
                        if d == 0:
                            o_, a_, b_ = HD[:, s0:s0 + Ls], A_d[:, s0:s0 + Ls], BX_d[:, s0:s0 + Ls]
                        else:
                            o_, a_, b_ = (HD[:, s0:s0 + Ls][:, ::-1], A_d[:, s0:s0 + Ls][:, ::-1],
                                          BX_d[:, s0:s0 + Ls][:, ::-1])
                        kb.op("dve", lambda e_: e_.tensor_tensor_scan(out=o_, data0=a_, data1=b_, initial=init,
                                                                      op0=ALU.mult, op1=ALU.add),
                              [A_d.R(), BX_d.R(), H0S.R()], [HD.R()])
                        if pname == "p":
                            col = s0 + Ls - 1 if d == 0 else s0
                            kb.cp(NST[:, e, si, d, n:n + 1], HD[:, col:col + 1], [HD.R()], [NST.R()])
                gring.release(gs)
                if do_ada:
                    ada_run(l + 1, range(n * 12 + 3, n * 12 + 6))
                if pname == "s" and l == 0:
                    ada_run(0, range(48 + n * 6, 48 + n * 6 + 6))
                s = ring.get(("win", pname, l, n, 1))
                ps = psget()
                proj_cols(s, Hh, T, ps)
                ring.release(s)
                kb.act(A_[:], ps[:, 0:T], AF.Gelu_apprx_tanh, [ps.R()], [A_.R()])
                kb.tt(HF[:], HF[:], HB[:], ALU.add, [HF.R(), HB.R()], [HF.R()])
                kb.tt(Y[:, n, :], HF[:], A_[:], ALU.mult, [HF.R(), A_.R()], [Y.R(n)])
                sa = ring.get(("win", pname, l, n, 2))
                psa = psget()
                proj_cols(sa, Hh, T, psa)
                ring.release(sa)
                sb_ = ring.get(("win", pname, l, n, 3))
                psb = psget()
                proj_cols(sb_, Hh, T, psb)
                ring.release(sb_)
                kb.act(HB[:], psb[:, 0:T], AF.Sigmoid, [psb.R()], [HB.R()])
                kb.tt(XCB[:], psa[:, 0:T], HB[:], ALU.mult, [psa.R(), HB.R()], [XCB.R()])
                if do_ada:
                    ada_run(l + 1, range(n * 12 + 6, n * 12 + 9))
                ccw = lambda k: P(f"ccw{e}", n, k)
                pc = psget()
                order = [15] + [k for k in range(31) if k != 15]
                for ti, k in enumerate(order):
                    d = k - 15
                    dg = DG[ti % 4]
                    kb.ts(dg[:], IDF[:], ccw(k), None, ALU.mult, None, [IDF.R(), PAR.R()], [dg.R()])
                    fns = []
                    if ti == 0:
                        for hf in range(T // 512):
                            fns.append(MM(pc[:, hf * 512:(hf + 1) * 512], dg[:], XCB[:, hf * 512:(hf + 1) * 512],
                                          True, False))
                    for (s0_, Ls) in (seqs if ti > 0 else []):
                        lo = s0_ + max(0, -d)
                        hi = s0_ + Ls - max(0, d)
                        for hf in range(T // 512):
                            a = max(lo, hf * 512)
                            b_ = min(hi, (hf + 1) * 512)
                            if b_ > a:
                                fns.append(MM(pc[:, a:b_], dg[:], XCB[:, a + d:b_ + d], ti == 0, ti == 30))
                    kb.mm(fns, [dg.R(), XCB.R()], [pc.R()])
                kb.act(Y[:, 8 + n, :], pc[:, 0:T], AF.Identity, [pc.R(), PAR.R()], [Y.R(8 + n)], bias=P(f"ccb{e}", n))
                if do_ada:
                    ada_run(l + 1, range(n * 12 + 9, n * 12 + 12))
            ps1 = psget()
            ps2 = psget()
            for n in range(8):
                for t in range(T // 512):
                    cs = slice(t * 512, (t + 1) * 512)
                    kb.mm([MM(ps1[:, cs], ONES[:], Y[:, 8 + n, cs], n == 0, n == 7)], [ONES.R(), Y.R(8 + n)], [ps1.R()])
                kb.act(XCB[:], Y[:, 8 + n, :], AF.Square, [Y.R(8 + n)], [XCB.R()])
                for t in range(T // 512):
                    cs = slice(t * 512, (t + 1) * 512)
                    kb.mm([MM(ps2[:, cs], ONES[:], XCB[:, cs], n == 0, n == 7)], [ONES.R(), XCB.R()], [ps2.R()])
            kb.act(A_[:], ps1[:, 0:T], AF.Copy, [ps1.R()], [A_.R()], scale=1.0 / 1024)
            kb.tt(HB[:], A_[:], A_[:], ALU.mult, [A_.R()], [HB.R()])
            kb.stt(BX[:], ps2[:, 0:T], 1.0 / 1024, HB[:], ALU.mult, ALU.subtract, [ps2.R(), HB.R()], [BX.R()])
            kb.act(BX[:], BX[:], AF.Sqrt, [BX.R(), EPSV.R()], [BX.R()], bias=EPSV[:, 0:1])
            kb.op("dve", lambda e_: e_.reciprocal(out=BX[:], in_=BX[:]), [BX.R()], [BX.R()])
            for n in range(8):
                kb.tt(HF[:], Y[:, 8 + n, :], A_[:], ALU.subtract, [Y.R(8 + n), A_.R()], [HF.R()])
                kb.tt(HF[:], HF[:], BX[:], ALU.mult, [HF.R(), BX.R()], [HF.R()])
                kb.act(Y[:, 8 + n, :], HF[:], AF.Silu, [HF.R(), PAR.R()], [Y.R(8 + n)],
                       scale=P(f"lng{e}", n), bias=P(f"lnb{e}", n))

        def odd_mixer(pname, l, T, seqs, Hh, Y, tes):
            o = l // 2
            NTC = T // 128
            QB = kb.sb("QB", [128, T], BF16, tes)
            KBF = kb.sb("KBF", [128, T], BF16, tes)
            VB = kb.sb("VB", [128, NTC, 128], BF16, tes)
            PBs = [kb.sb("PB", [128, NKS], BF16, tes) for _ in range(2)]
            PTs = [kb.sb("PT", [128, 7, 128], BF16, tes) for _ in range(2)]
            MXs = [kb.sb("MX", [128, 4], F32, tes) for _ in range(2)]
            RDs = [kb.sb("RD", [128, 128], F32, tes) for _ in range(2)]
            if pname == "p":
                KF = kb.sb("KF", [128, 512], F32, tes)
                VF = kb.sb("VF", [128, 4, 128], F32, tes)
            elif "units" not in SKIP:
                tring = Ring(kb, 3, [128, NKW], BF16, "tring", tes)
                tring.plan = [p_ for p_ in tplan if p_[0][1] == l]
                kring = Ring(kb, 2, [128, 256], BF16, "kring", tes)
                kring.plan = [p_ for p_ in kplan if p_[0][1] == l]
                vring = Ring(kb, 2, [128, 256], BF16, "vring", tes)
                vring.plan = [p_ for p_ in vplan if p_[0][1] == l]
            for hd in range(NH):
                s = ring.get(("wqkv", pname, l, hd, 0))
                ps = psget(2)
                proj_cols(s, Hh, T, ps)
                ring.release(s)
                kb.act(QB[:], ps[:, 0:T], AF.Copy, [ps.R()], [QB.R()], scale=float(128 ** -0.5))
                do_ada = pname == "s" and l + 1 < nl
                if do_ada:
                    ada_run(l + 1, range(hd * 6, hd * 6 + 2), 2)
                s = ring.get(("wqkv", pname, l, hd, 1))
                ps = psget(2)
                proj_cols(s, Hh, T, ps)
                ring.release(s)
                kb.act(KBF[:], ps[:, 0:T], AF.Copy, [ps.R()], [KBF.R()])
                if pname == "p" and "kvout" not in SKIP:
                    kb.act(KF[:], ps[:, 0:512], AF.Copy, [ps.R()], [KF.R()])
                    kb.dma("sp", nk_d[o, :, hd, :], KF[:], ds_out, reads=[KF.R()])
                if do_ada:
                    ada_run(l + 1, range(hd * 6 + 2, hd * 6 + 4), 2)
                s = ring.get(("wqkv", pname, l, hd, 2))
                s3 = slot3(s)
                ps = psget(2)
                for tc in range(NTC if "vproj" not in SKIP else 0):
                    kb.mm([MM(ps[:, tc * 128:(tc + 1) * 128], Hh[:, k, tc * 128:(tc + 1) * 128], s3[:, k, :],
                              k == 0, k == 15) for k in range(16)], [s.R()] + [Hh.R(k) for k in range(16)], [ps.R()])
                ring.release(s)
                kb.act(VB[:].rearrange("p c d -> p (c d)"), ps[:, 0:T], AF.Copy, [ps.R()], [VB.R()])
                if pname == "p" and "kvout" not in SKIP:
                    kb.act(VF[:].rearrange("p c d -> p (c d)"), ps[:, 0:512], AF.Copy, [ps.R()], [VF.R()])
                    kb.dma("sp", nv_d[o, :, :, hd * 128:(hd + 1) * 128], VF[:], ds_out, reads=[VF.R()])
                if do_ada:
                    ada_run(l + 1, range(hd * 6 + 4, hd * 6 + 6), 2)
                if "units" in SKIP:
                    units = []
                elif pname == "p":
                    units = [(s0 + qq * 128, s0, None) for (s0, Ls) in seqs for qq in range(Ls // 128)]
                else:
                    units = [(qt * 128, QT_R0[qt] * 64, qt) for qt in range(8)]
                    kc_ = kring.get(("kc", l, hd))
                    vc_ = vring.get(("vc", l, hd))
                    vc3 = vc_.t[:].rearrange("p (c d) -> p c d", c=2)

                def stageA(i):
                    q0, k0, qt = units[i]
                    ps = psget(2)
                    qs = QB[:, q0:q0 + 128]
                    if qt is None:
                        kb.mm([MM(ps[:, 0:256], qs, KBF[:, k0:k0 + 256])], [QB.R(), KBF.R()], [ps.R()])
                    else:
                        tb = tring.get(("bt", l, hd, qt))
                        kb.mm([MM(ps[:, 0:512], qs, KBF[:, k0:k0 + 512], True, False),
                               MM(ps[:, 0:512], IDB[:], tb.t[:, 0:512], False, True),
                               MM(ps[:, 512:640], qs, KBF[:, k0 + 512:k0 + 640], True, False),
                               MM(ps[:, 512:640], IDB[:], tb.t[:, 512:640], False, True),
                               MM(ps[:, 640:896], qs, kc_.t[:, :])],
                              [QB.R(), KBF.R(), IDB.R(), tb.R(), kc_.R()], [ps.R()])
                        tring.release(tb)
                    return ps

                def stageB(i, ps):
                    q0, k0, qt = units[i]
                    nks = 256 if qt is None else NKS
                    PB, MX = PBs[i % 2], MXs[i % 2]
                    kb.op("dve", lambda e_: e_.tensor_reduce(out=MX[:, 0:1], in_=ps[:, 0:nks], axis=AX.X, op=ALU.max),
                          [ps.R()], [MX.R()])
                    kb.ts(MX[:, 1:2], MX[:, 0:1], -1.0, None, ALU.mult, None, [MX.R()], [MX.R()])
                    kb.act(PB[:, 0:nks], ps[:, 0:nks], AF.Exp, [ps.R(), MX.R()], [PB.R()], bias=MX[:, 1:2])

                def stageC1(i):
                    q0, k0, qt = units[i]
                    nch = (256 if qt is None else NKS) // 128
                    PB, PT = PBs[i % 2], PTs[i % 2]
                    pst = PSH[4 + i % 2]
                    pstb = pst.ap.bitcast(BF16)
                    kb.mm([(lambda c: lambda pe: pe.transpose(out=pstb[:, c * 128:(c + 1) * 128],
                                                               in_=PB[:, c * 128:(c + 1) * 128], identity=IDB[:]))(c)
                           for c in range(nch)], [PB.R(), IDB.R()], [pst.R()])
                    kb.act(PT[:].rearrange("p c q -> p (c q)")[:, 0:nch * 128], pstb[:, 0:nch * 128], AF.Copy,
                           [pst.R()], [PT.R()])

                def stageC2(i):
                    q0, k0, qt = units[i]
                    nch = (256 if qt is None else NKS) // 128
                    PT = PTs[i % 2]
                    RD = RDs[i % 2]
                    pso = PSH[6 + i % 2]
                    fns = []
                    rds = [PT.R(), VB.R(), ONES.R()]
                    for c in range(nch):
                        if qt is None or c < 5:
                            vch = VB[:, k0 // 128 + c, :]
                        else:
                            vch = vc3[:, c - 5, :]
                        fns.append(MM(pso[:, 0:128], vch, PT[:, c, :], c == 0, c == nch - 1))
                    for c in range(nch):
                        fns.append(MM(pso[:, 128:256], ONES[:], PT[:, c, :], c == 0, c == nch - 1))
                    if qt is not None:
                        rds.append(vc_.R())
                    kb.mm(fns, rds, [pso.R()])
                    kb.op("dve", lambda e_: e_.reciprocal(out=RD[:], in_=pso[:, 128:256]), [pso.R()], [RD.R()])
                    kb.tt(Y[:, hd, q0:q0 + 128], pso[:, 0:128], RD[:], ALU.mult, [pso.R(), RD.R()], [Y.R(hd)])

                nu = len(units)
                for i in range(-1, nu):
                    if i + 1 < nu:
                        ps_ = stageA(i + 1)
                        stageB(i + 1, ps_)
                    if i >= 0:
                        stageC1(i)
                    if i >= 1:
                        stageC2(i - 1)
                if nu:
                    stageC2(nu - 1)
                if pname == "s" and "units" not in SKIP:
                    kring.release(kc_)
                    vring.release(vc_)

        def ffn(pname, l, T, seqs, X, cnd):
            with ExitStack() as fs:
                HFN = kb.sb("HFN", [128, KC, T], BF16, fs)
                ACC = kb.sb("ACC", [128, KC, 512], F32, fs)
                ACTB = [kb.sb(f"ACTB{i}", [128, GRP, 512], BF16, fs) for i in range(2)]
                CGV = [kb.sb("CG", [128, 512], F32, fs), kb.sb("CV", [128, 512], F32, fs)]
                nt = NormTmp(fs)
                derive_ffn(l, cnd)
                premod(X, HFN, T, DER[:, 2, :], MODS[:, l, cnd, 48:64], [MODS.R((l, cnd))], nt, pshget)
                cc = [0]
                starts = [s0 for (s0, Ls) in seqs]
                NW = T // 512
                st = {}

                def wave_geom(w):
                    c0 = w * 512
                    segs = []
                    for (s0, Ls) in seqs:
                        a = max(s0, c0)
                        b_ = min(s0 + Ls, c0 + 512)
                        if b_ > a:
                            segs.append((a - c0, b_ - c0))
                    lh = c0 - 1 if (c0 > 0 and c0 not in starts) else None
                    rh = c0 + 512 if (c0 + 512 < T and (c0 + 512) not in starts) else None
                    return c0, segs, lh, rh

                def chunk_front(w, j):
                    c0, segs, lh, rh = wave_geom(w)
                    pgv = (PSH[2 * (cc[0] % 2)], PSH[2 * (cc[0] % 2) + 1])
                    HT = PSH[6 + cc[0] % 2]
                    cc[0] += 1
                    hcol = (hc[0] % 128) * 4
                    hc[0] += 1
                    halo = lh is not None or rh is not None
                    for part in range(2):
                        s_ = ring.get(("wup", pname, l, w, j, part))
                        s3 = slot3(s_)
                        pp = pgv[part]
                        fns = []
                        for k in range(16):
                            fns.append(MM(pp[:, 0:512], s3[:, k, :], HFN[:, k, c0:c0 + 512], k == 0, k == 15))
                            if lh is not None:
                                fns.append(MM(HT[:, hcol + part * 2:hcol + part * 2 + 1], s3[:, k, :],
                                              HFN[:, k, lh:lh + 1], k == 0, k == 15))
                            if rh is not None:
                                fns.append(MM(HT[:, hcol + part * 2 + 1:hcol + part * 2 + 2], s3[:, k, :],
                                              HFN[:, k, rh:rh + 1], k == 0, k == 15))
                        kb.mm(fns, [s_.R()] + [HFN.R(k) for k in range(16)], [pp.R()] + ([HT.R()] if halo else []))
                        ring.release(s_)
                    st[(w, j)] = (pgv, HT, hcol)

                def chunk_back(w, j, AB, jj):
                    c0, segs, lh, rh = wave_geom(w)
                    pgv, HT, hcol = st.pop((w, j))
                    for part in range(2):
                        pp = pgv[part]
                        CB = CGV[part]
                        c = part * NFF + j
                        u = pp[:, 0:512]
                        w_ = lambda k: P(f"fcw{l}", c, k)
                        kb.act(CB[:], u, AF.Identity, [pp.R(), PAR.R()], [CB.R()], scale=w_(1), bias=P(f"fcb{l}", c))
                        for (a, b_) in segs:
                            if b_ - a > 1:
                                kb.stt(CB[:, a + 1:b_], u[:, a:b_ - 1], w_(0), CB[:, a + 1:b_], ALU.mult, ALU.add,
                                       [pp.R(), CB.R(), PAR.R()], [CB.R()])
                                kb.stt(CB[:, a:b_ - 1], u[:, a + 1:b_], w_(2), CB[:, a:b_ - 1], ALU.mult, ALU.add,
                                       [pp.R(), CB.R(), PAR.R()], [CB.R()])
                        if lh is not None:
                            kb.stt(CB[:, 0:1], HT[:, hcol + part * 2:hcol + part * 2 + 1], w_(0), CB[:, 0:1],
                                   ALU.mult, ALU.add, [HT.R(), CB.R(), PAR.R()], [CB.R()])
                        if rh is not None:
                            kb.stt(CB[:, 511:512], HT[:, hcol + part * 2 + 1:hcol + part * 2 + 2], w_(2),
                                   CB[:, 511:512], ALU.mult, ALU.add, [HT.R(), CB.R(), PAR.R()], [CB.R()])
                    kb.act(CGV[0][:], CGV[0][:], AF.Gelu_apprx_tanh, [CGV[0].R()], [CGV[0].R()])
                    kb.tt(AB[:, jj, :], CGV[0][:], CGV[1][:], ALU.mult, [CGV[0].R(), CGV[1].R()], [AB.R(jj)])

                for w in range(NW):
                    for g in range(NFF // GRP):
                        AB = ACTB[g % 2]
                        for jj in range(GRP):
                            j = g * GRP + jj
                            if (w, j) not in st:
                                chunk_front(w, j)
                            chunk_back(w, j, AB, jj)
                        sds = [ring.get(("wdn", pname, l, w, g * GRP + jj)) for jj in range(GRP)]
                        for m in range(16):
                            ps = pshget()
                            kb.mm([MM(ps[:, 0:512], sds[jj].t[:, m * 128:(m + 1) * 128], AB[:, jj, :], jj == 0,
                                      jj == GRP - 1) for jj in range(GRP)],
                                  [sd.R() for sd in sds] + [AB.R(jj) for jj in range(GRP)], [ps.R()])
                            if g == 0:
                                kb.cp(ACC[:, m, :], ps[:, 0:512], [ps.R()], [ACC.R(m)])
                            else:
                                kb.tt(ACC[:, m, :], ACC[:, m, :], ps[:, 0:512], ALU.add, [ACC.R(m), ps.R()], [ACC.R(m)])
                        for sd in sds:
                            ring.release(sd)
                    if w + 1 < NW:
                        chunk_front(w + 1, 0)
                        chunk_front(w + 1, 1)
                    post_residual(ACC, X, slice(w * 512, (w + 1) * 512), DER[:, 3, :], nt, pshget)
                kb.barrier()

        for pname, T, seqs in passes:
            cnd = 0 if pname == "p" else 1
            with ExitStack() as pes:
                X = kb.sb("X" + pname, [128, KC, T], F32, pes)
                xsrc = xp_d if pname == "p" else xs_d
                for k in range(KC):
                    kb.dma("sp", X[:, k, :], xsrc[:, k, :], ds_x, writes=[X.R(k)])
                for k in range(KC):
                    X.R(k).w = ("d", ds_x, ds_x["c"])
                for l in range(nl):
                    derive(l, cnd)
                    with ExitStack() as ys:
                        Y = kb.sb("Y", [128, KC, T], BF16, ys)
                        with ExitStack() as hs:
                            Hh = kb.sb("H", [128, KC, T], BF16, hs)
                            with ExitStack() as ns:
                                nt = NormTmp(ns)
                                premod(X, Hh, T, DER[:, 0, :], MODS[:, l, cnd, 0:16], [MODS.R((l, cnd))], nt)
                                kb.barrier()
                            with ExitStack() as tes:
                                if l % 2 == 0:
                                    even_mixer(pname, l, T, seqs, Hh, Y, tes)
                                else:
                                    odd_mixer(pname, l, T, seqs, Hh, Y, tes)
                                kb.barrier()
                        with ExitStack() as os_:
                            OB = kb.sb("OB", [128, KC, 512], F32, os_)
                            nt = NormTmp(os_)
                            outproj_residual(pname, l, T, X, Y, OB, nt)
                            kb.barrier()
                    ffn(pname, l, T, seqs, X, cnd)
                ydst = yp_d if pname == "p" else ys_d
                for k in range(KC):
                    kb.dma("sp", ydst[:, k, :], X[:, k, :], ds_out, reads=[X.R(k)])
                if pname == "p":
                    for e in range(2):
                        kb.dma("sp", nst_d[e], NST[:, e].rearrange("p s d n -> p (s d n)"), ds_out, reads=[NST.R()])
                kb.barrier()
        kb.barrier()
    return nc


_PROG = {}


def _btab(rel):
    tab = np.full((2, NH, 5, 128, NKW), -1e30, np.float32)
    for vi, qt in enumerate([0, 1, 2, 6, 7]):
        r0, R0 = 2 * qt, QT_R0[qt]
        q = np.arange(128)
        r = r0 + q // 64
        c = q % 64
        kk = np.arange(NKW)
        kr = R0 + kk // 64
        kc = kk % 64
        rsr = np.clip(r - 4, 0, 8)
        csc = np.clip(c - 8, 0, 48)
        valid = ((kr[None, :] >= rsr[:, None]) & (kr[None, :] < rsr[:, None] + 8) &
                 (kc[None, :] >= csc[:, None]) & (kc[None, :] < csc[:, None] + 16))
        dr = np.clip(kr[None, :] - r[:, None] + 7, 0, 14)
        dc = np.clip(kc[None, :] - c[:, None], -15, 15) + 15
        vals = rel[:, :, dr, dc]
        tab[:, :, vi] = np.where(valid[None, None], vals, np.float32(-1e30))
    return tab


def kernel(**inp):
    nl = int(os.environ.get("MK_NL", L))
    g = {k: np.asarray(v) for k, v in inp.items()}
    poff, NP = param_layout()
    par = np.zeros((128, NP), np.float32)

    def put(name, arr):
        o, shape = poff[name]
        par[:, o:o + int(np.prod(shape))] = np.ascontiguousarray(arr).reshape(128, -1)
    for l in range(L):
        put(f"b_ada{l}", g["b_ada"][l].reshape(96, 128).T)
        put(f"norm_g{l}", g["norm_g"][l].reshape(4, 16, 128).transpose(2, 0, 1))
        put(f"fcw{l}", g["ffn_conv_w"][l].reshape(3, 88, 128).transpose(2, 1, 0))
        put(f"fcb{l}", g["ffn_conv_b"][l].reshape(88, 128).T)
    for e in range(2):
        put(f"rcw{e}", g["rg_conv_w"][e].reshape(4, 8, 128).transpose(2, 1, 0))
        put(f"rcb{e}", g["rg_conv_b"][e].reshape(8, 128).T)
        put(f"rbg{e}", g["rg_b_gate"][e].reshape(2, 2, 8, 128).transpose(3, 0, 1, 2))
        put(f"lam{e}", g["rg_lambda"][e].reshape(2, 8, 128).transpose(2, 0, 1))
        put(f"ccw{e}", g["cf_conv_w"][e].reshape(31, 8, 128).transpose(2, 1, 0))
        put(f"ccb{e}", g["cf_conv_b"][e].reshape(8, 128).T)
        put(f"lng{e}", g["cf_ln_g"][e].reshape(8, 128).T)
        put(f"lnb{e}", g["cf_ln_b"][e].reshape(8, 128).T)
    btab = _btab(g["na_rel_bias"])
    ident = np.eye(128, dtype=np.float32)
    shared = {"par": par, "btab": btab, "ident": ident}
    def tile_cols(W):
        lead = W.shape[:-2]
        N = W.shape[-1]
        nl_ = len(lead)
        Wt = W.reshape(*lead, 16, 128, N // 128, 128)
        perm = tuple(range(nl_)) + (nl_ + 2, nl_ + 1, nl_ + 0, nl_ + 3)
        return np.ascontiguousarray(Wt.transpose(perm)).reshape(*lead, N // 128, 128, 16 * 128)
    for k in ("w_ada", "w_in_even", "w_out_even", "w_qkv_odd", "w_out_odd", "w_ffn_up"):
        shared[k] = tile_cols(np.asarray(g[k], dtype=np.float32))
    for k in ("rg_w_gate", "w_ffn_down"):
        shared[k] = np.ascontiguousarray(g[k], dtype=np.float32)

    def fm(x):
        return np.ascontiguousarray(x.reshape(x.shape[0], KC, 128).transpose(2, 1, 0))
    in_maps = []
    for i in range(NCORES):
        m = dict(shared)
        m["xp"] = fm(g["x_prompt"][2 * i:2 * i + 2].reshape(512, D))
        m["xs"] = fm(g["x_sample"][i])
        cond = np.stack([g["c_ctx"], g["c"][i]], axis=-1)
        m["cond"] = np.ascontiguousarray(cond.reshape(KC, 128, 2).transpose(1, 0, 2))
        m["kctx"] = np.ascontiguousarray(g["cache_k"][i].transpose(0, 3, 2, 1))
        m["vctx"] = np.ascontiguousarray(g["cache_v"][i].reshape(2, 2, 128, D).transpose(0, 2, 1, 3))
        m["h0"] = np.ascontiguousarray(g["state_rglru"][i].reshape(2, 2, 8, 128).transpose(0, 3, 1, 2))
        in_maps.append(m)
    key = nl
    if key not in _PROG:
        mp = os.environ.get("MK_PASS", "ps")
        _PROG[key] = build_program(nl, do_sample="s" in mp, do_prompt="p" in mp)
    ncr = int(os.environ.get("MK_CORES", NCORES))
    res = run_bass_kernel_spmd(_PROG[key], in_maps[:ncr], core_ids=list(range(ncr)))
    yp = np.zeros((16, 256, D), np.float32)
    ys = np.zeros((8, 1024, D), np.float32)
    nk = np.zeros((16, 2, 256, NH, 128), np.float32)
    nv = np.zeros((16, 2, 256, NH, 128), np.float32)
    nst = np.zeros((16, 2, 2, 1024), np.float32)
    for i, r in enumerate(res.results):
        yp[2 * i:2 * i + 2] = r["yp"].transpose(2, 1, 0).reshape(2, 256, D)
        ys[i] = r["ys"].transpose(2, 1, 0).reshape(1024, D)
        k_ = r["nk"].transpose(0, 3, 2, 1).reshape(2, 2, 256, NH, 128)
        nk[2 * i:2 * i + 2] = k_.transpose(1, 0, 2, 3, 4)
        v_ = r["nv"].transpose(0, 2, 1, 3).reshape(2, 2, 256, NH, 128)
        nv[2 * i:2 * i + 2] = v_.transpose(1, 0, 2, 3, 4)
        s_ = r["nst"].transpose(2, 0, 3, 4, 1).reshape(2, 2, 2, 1024)
        nst[2 * i:2 * i + 2] = s_
    return yp, ys, nk, nv, nst
```
